# Optimizing a Trainium2 kernel written in Bass

```python
import jax, jax.numpy as jnp
from jax import lax
import numpy as np

D_MODEL = 2048
BATCH = 8
SEQ = 2048
DEPTH = 1
DEC_BATCH = 16
DEC_SEQ = 32
PAST_LEN = 2048

CHUNK = 64
D_M = D_MODEL // 2
H_M = 4
DH_M = D_M // H_M
D_H = D_MODEL // 2
DH_H = 128
H_H = D_H // DH_H
CONV_W = 4
D_FF = 4 * D_MODEL
EPS = 1e-6

kernel_name = "hybrid_mlstm_hgrn2_streaming_step"


def _split_points():
    sizes = [2 * D_M, D_M, D_M, 2 * H_M, D_H, D_H, D_H, D_H, D_MODEL, D_MODEL]
    pts, acc = [], 0
    for s in sizes[:-1]:
        acc += s
        pts.append(acc)
    return pts, acc + sizes[-1]


def _rmsnorm(x, g):
    xf = x.astype(jnp.float32)
    y = xf * lax.rsqrt(jnp.mean(xf * xf, axis=-1, keepdims=True) + EPS)
    return (y * g.astype(jnp.float32)).astype(x.dtype)


def _causal_conv(x_pad, w, b):
    T = x_pad.shape[1] - CONV_W + 1
    out = b
    for i in range(CONV_W):
        out = out + x_pad[:, i:i + T] * w[i]
    return out


def _to_blocks(a, L):
    B, T = a.shape[0], a.shape[1]
    a = a.reshape((B, T // L, L) + a.shape[2:])
    perm = (1, 0, 3, 2) + tuple(range(4, a.ndim))
    return a.transpose(perm)


def _from_blocks(a):
    NC, B, H, L, d = a.shape
    return a.transpose(1, 0, 3, 2, 4).reshape(B, NC * L, H, d)


def _mlstm_chunkwise(q, k, v, ig, lf, C0, n0, m0):
    T = q.shape[1]
    L = min(CHUNK, T)
    causal = jnp.tril(jnp.ones((L, L), dtype=bool))

    def step(carry, inp):
        C, n, m = carry
        qc, kc, vc, igc, lfc = inp
        bcum = jnp.cumsum(lfc, axis=-1)
        dlog = bcum[..., :, None] - bcum[..., None, :] + igc[..., None, :]
        dlog = jnp.where(causal, dlog, -jnp.inf)
        m_inter = bcum + m[..., None]
        m_t = jnp.maximum(m_inter, jnp.max(dlog, axis=-1))
        s = jnp.einsum('bhld,bhsd->bhls', qc, kc) * jnp.exp(dlog - m_t[..., None])
        sc_inter = jnp.exp(m_inter - m_t)
        num = sc_inter[..., None] * jnp.einsum('bhvd,bhld->bhlv', C, qc) + jnp.einsum('bhls,bhsv->bhlv', s, vc)
        den = sc_inter * jnp.einsum('bhd,bhld->bhl', n, qc) + jnp.sum(s, axis=-1)
        h = num / jnp.maximum(jnp.abs(den), jnp.exp(-m_t))[..., None]
        m_new = m_t[..., -1]
        decay = jnp.exp(bcum[..., -1] + m - m_new)
        wgt = jnp.exp(bcum[..., -1:] - bcum + igc - m_new[..., None])
        C_new = decay[..., None, None] * C + jnp.einsum('bhs,bhsv,bhsd->bhvd', wgt, vc, kc)
        n_new = decay[..., None] * n + jnp.einsum('bhs,bhsd->bhd', wgt, kc)
        return (C_new, n_new, m_new), h

    xs = (_to_blocks(q, L), _to_blocks(k, L), _to_blocks(v, L), _to_blocks(ig, L), _to_blocks(lf, L))
    (C1, n1, m1), hb = lax.scan(step, (C0, n0, m0), xs)
    return _from_blocks(hb), C1, n1, m1


def _hgrn2_chunkwise(q, k, v, lf, S0):
    T = q.shape[1]
    L = min(CHUNK, T)
    causal = jnp.tril(jnp.ones((L, L), dtype=bool))[:, :, None]

    def step(S, inp):
        qc, kc, vc, lfc = inp
        a = jnp.cumsum(lfc, axis=2)
        diff = a[:, :, :, None, :] - a[:, :, None, :, :]
        dec = jnp.exp(jnp.where(causal, diff, -jnp.inf))
        scores = jnp.einsum('bhlc,bhsc,bhlsc->bhls', qc, kc, dec)
        o = jnp.einsum('bhlc,bhcv->bhlv', qc * jnp.exp(a), S) + jnp.einsum('bhls,bhsv->bhlv', scores, vc)
        aL = a[:, :, -1]
        S_new = jnp.exp(aL)[..., None] * S + jnp.einsum('bhsc,bhsv->bhcv', kc * jnp.exp(aL[:, :, None] - a), vc)
        return S_new, o

    xs = (_to_blocks(q, L), _to_blocks(k, L), _to_blocks(v, L), _to_blocks(lf, L))
    S1, ob = lax.scan(step, S0, xs)
    return _from_blocks(ob), S1


def _layer(x, conv_prev, C0, n0, m0, S0, lb, g_mix, w_in, b_if, w_conv, b_conv, g_mnorm, g_hnorm,
           w_branch_a, w_branch_b, w_out, g_ffn, w_ff1, w_ff2):
    f32 = jnp.float32
    B, T, _ = x.shape
    pts, _ = _split_points()
    h = _rmsnorm(x, g_mix)
    proj = h @ w_in
    qk_pre, v_m, o_m, if_pre, f_h, i_h, q_h, g_h, gate_a, gate_b = jnp.split(proj, pts, axis=-1)

    qk_pad = jnp.concatenate([conv_prev.astype(qk_pre.dtype), qk_pre], axis=1)
    new_conv = qk_pad[:, -(CONV_W - 1):]
    qk = jax.nn.silu(_causal_conv(qk_pad, w_conv, b_conv))
    q_m, k_m = jnp.split(qk.astype(f32), 2, axis=-1)
    q_m = q_m.reshape(B, T, H_M, DH_M)
    k_m = k_m.reshape(B, T, H_M, DH_M) * (DH_M ** -0.5)
    vm = v_m.astype(f32).reshape(B, T, H_M, DH_M)
    ig, fg = jnp.split((if_pre + b_if).astype(f32), 2, axis=-1)
    lf_m = jax.nn.log_sigmoid(fg)
    hm, C1, n1, m1 = _mlstm_chunkwise(q_m, k_m, vm, ig, lf_m, C0, n0, m0)
    hm = _rmsnorm(hm, g_mnorm).reshape(B, T, D_M) * jax.nn.sigmoid(o_m.astype(f32))
    y_a = hm.astype(x.dtype) @ w_branch_a

    sig_f = jax.nn.sigmoid(f_h.astype(f32)).reshape(B, T, H_H, DH_H)
    f = lb + (1.0 - lb) * sig_f
    lf_h = jnp.log(f)
    k_h = 1.0 - f
    qh = q_h.astype(f32).reshape(B, T, H_H, DH_H)
    vh = i_h.astype(f32).reshape(B, T, H_H, DH_H)
    oh, S1 = _hgrn2_chunkwise(qh, k_h, vh, lf_h, S0)
    oh = _rmsnorm(oh.reshape(B, T, D_H), g_hnorm) * jax.nn.silu(g_h.astype(f32))
    y_b = oh.astype(x.dtype) @ w_branch_b

    u = jax.nn.sigmoid(gate_a) * y_a + jax.nn.sigmoid(gate_b) * y_b
    x = x + u @ w_out

    h2 = _rmsnorm(x, g_ffn)
    x = x + jnp.square(jax.nn.relu(h2 @ w_ff1)) @ w_ff2
    return x, new_conv, C1, n1, m1, S1


def setup_inputs(seed: int = 0) -> dict:
    key = jax.random.key(seed)
    ks = jax.random.split(key, 24)
    _, n_in = _split_points()
    nrm = jax.random.normal
    b_if = jnp.concatenate([
        0.1 * nrm(ks[7], (DEPTH, H_M)),
        jnp.broadcast_to(jnp.linspace(3.0, 6.0, H_M), (DEPTH, H_M)) + 0.1 * nrm(ks[8], (DEPTH, H_M)),
    ], axis=-1)
    return {
        "x_prompt": nrm(ks[0], (BATCH, SEQ, D_MODEL), jnp.float32),
        "x_sample": nrm(ks[1], (DEC_BATCH, DEC_SEQ, D_MODEL), jnp.float32),
        "cache_mlstm_conv": nrm(ks[2], (DEPTH, DEC_BATCH, CONV_W - 1, 2 * D_M), jnp.float32),
        "state_mlstm_C": 0.3 * nrm(ks[3], (DEPTH, DEC_BATCH, H_M, DH_M, DH_M), jnp.float32),
        "state_mlstm_n": 0.3 * nrm(ks[4], (DEPTH, DEC_BATCH, H_M, DH_M), jnp.float32),
        "state_mlstm_m": 0.5 * nrm(ks[5], (DEPTH, DEC_BATCH, H_M), jnp.float32),
        "state_hgrn_S": 0.3 * nrm(ks[6], (DEPTH, DEC_BATCH, H_H, DH_H, DH_H), jnp.float32),
        "g_mix": 1.0 + 0.02 * nrm(ks[9], (DEPTH, D_MODEL)),
        "w_in": nrm(ks[10], (DEPTH, D_MODEL, n_in)) * D_MODEL ** -0.5,
        "b_if": b_if,
        "w_conv": nrm(ks[11], (DEPTH, CONV_W, 2 * D_M)) * CONV_W ** -0.5,
        "b_conv": 0.02 * nrm(ks[12], (DEPTH, 2 * D_M)),
        "g_mnorm": 1.0 + 0.02 * nrm(ks[13], (DEPTH, H_M, DH_M)),
        "g_hnorm": 1.0 + 0.02 * nrm(ks[14], (DEPTH, D_H)),
        "hgrn_lb_logits": 0.5 * nrm(ks[15], (DEPTH + 1, D_H)),
        "w_branch_a": nrm(ks[16], (DEPTH, D_M, D_MODEL)) * D_M ** -0.5,
        "w_branch_b": nrm(ks[17], (DEPTH, D_H, D_MODEL)) * D_H ** -0.5,
        "w_out": nrm(ks[18], (DEPTH, D_MODEL, D_MODEL)) * D_MODEL ** -0.5,
        "g_ffn": 1.0 + 0.02 * nrm(ks[19], (DEPTH, D_MODEL)),
        "w_ff1": nrm(ks[20], (DEPTH, D_MODEL, D_FF)) * D_MODEL ** -0.5,
        "w_ff2": nrm(ks[21], (DEPTH, D_FF, D_MODEL)) * D_FF ** -0.5,
        "g_final": 1.0 + 0.02 * nrm(ks[22], (D_MODEL,)),
    }


def reference(x_prompt, x_sample, cache_mlstm_conv, state_mlstm_C, state_mlstm_n, state_mlstm_m, state_hgrn_S,
              g_mix, w_in, b_if, w_conv, b_conv, g_mnorm, g_hnorm, hgrn_lb_logits, w_branch_a, w_branch_b,
              w_out, g_ffn, w_ff1, w_ff2, g_final):
    f32 = jnp.float32
    Bp = x_prompt.shape[0]
    lb_table = jnp.cumsum(jax.nn.softmax(hgrn_lb_logits.astype(f32), axis=0), axis=0)
    xp, xs = x_prompt, x_sample
    pc, pC, pn, pm, pS = [], [], [], [], []
    sc, sC, sn, sm, sS = [], [], [], [], []
    for l in range(DEPTH):
        lb = lb_table[l].reshape(H_H, DH_H)
        w = (g_mix[l], w_in[l], b_if[l], w_conv[l], b_conv[l], g_mnorm[l], g_hnorm[l],
             w_branch_a[l], w_branch_b[l], w_out[l], g_ffn[l], w_ff1[l], w_ff2[l])
        xp, c1, C1, n1, m1, S1 = _layer(
            xp, jnp.zeros((Bp, CONV_W - 1, 2 * D_M), xp.dtype),
            jnp.zeros((Bp, H_M, DH_M, DH_M), f32), jnp.zeros((Bp, H_M, DH_M), f32),
            jnp.zeros((Bp, H_M), f32), jnp.zeros((Bp, H_H, DH_H, DH_H), f32), lb, *w)
        pc.append(c1); pC.append(C1); pn.append(n1); pm.append(m1); pS.append(S1)
        xs, c2, C2, n2, m2, S2 = _layer(
            xs, cache_mlstm_conv[l], state_mlstm_C[l].astype(f32), state_mlstm_n[l].astype(f32),
            state_mlstm_m[l].astype(f32), state_hgrn_S[l].astype(f32), lb, *w)
        sc.append(c2); sC.append(C2); sn.append(n2); sm.append(m2); sS.append(S2)
    y_prompt = _rmsnorm(xp, g_final)
    y_sample = _rmsnorm(xs, g_final)
    return (y_prompt, y_sample,
            jnp.stack(pc), jnp.stack(pC), jnp.stack(pn), jnp.stack(pm), jnp.stack(pS),
            jnp.stack(sc), jnp.stack(sC), jnp.stack(sn), jnp.stack(sm), jnp.stack(sS))
```

```python
import numpy as np
from contextlib import ExitStack
import concourse.bass as bass
import concourse.mybir as mybir
from concourse.bass_utils import run_bass_kernel_spmd

F32 = mybir.dt.float32
BF16 = mybir.dt.bfloat16
AF = mybir.ActivationFunctionType
ALU = mybir.AluOpType

D = 2048
NCORE = 8
SEQ = 2048
N_IN = 12296
EPS = 1e-6
C_QK, C_V, C_OM, C_IF, C_FH, C_IH, C_QH, C_GH, C_GA, C_GB = 0, 2048, 3072, 4096, 4104, 5128, 6152, 7176, 8200, 10248
NPART = 4
ML_ORDER = [("K", 0), ("V", 0), ("K", 1), ("OM", 0), ("Q", 0), ("V", 1), ("Q", 1), ("OM", 1)]
NSLOT = 3
ARENA_F32 = 22784

CO_ID, CO_MNEG, CO_M01, CO_SEL, CO_BM = 0, 128, 256, 384, 896
CW = 912
PO_GMIX, PO_GFFN, PO_GHN, PO_WCONV, PO_BCONV, PO_LBL, PO_BIF = 0, 16, 32, 40, 104, 120, 136
PW = 138


class Res:
    __slots__ = ("name", "lw", "rd", "excl", "bank")

    def __init__(self, name, excl=False, bank=-1):
        self.name = name
        self.lw = None
        self.rd = {}
        self.excl = excl
        self.bank = bank


class Sched:
    def __init__(self, nc, es):
        self.nc = nc
        self.es = es
        self.eng = {"pe": nc.tensor, "act": nc.scalar, "dve": nc.vector, "pool": nc.gpsimd, "sp": nc.sync}
        self.sem = {}
        self.cnt = {}
        for e in ("pe", "act", "dve"):
            self.sem[e] = es.enter_context(nc.semaphore("sem_" + e))
            self.cnt[e] = 0
        self.seen = {e: {} for e in self.eng}
        self.dsem = {}
        self.dcnt = {}
        self.nowait_dma = set()
        self.marks = []
        self.label = ""

    def _wait(self, e, tok):
        key, sem, val = tok
        if self.seen[e].get(key, 0) >= val:
            return
        self.eng[e].wait_ge(sem, val)
        self.seen[e][key] = val

    def _deps(self, e, reads, writes):
        deps = {}

        def add(tok, kind):
            if tok is None:
                return
            key = tok[0]
            if key == e and e == "pe":
                return
            if key not in deps or deps[key][2] < tok[2]:
                deps[key] = tok

        for r in reads:
            add(r.lw, "raw")
        for r in writes:
            add(r.lw, "waw")
            for t in r.rd.values():
                add(t, "war")
        for tok in deps.values():
            self._wait(e, tok)

    def _commit(self, tok, reads, writes):
        for r in reads:
            r.rd[tok[0]] = tok
        for r in writes:
            r.lw = tok
            r.rd = {}

    def op(self, e, fn, reads=(), writes=()):
        if any(r.excl for r in reads):
            writes = list(writes) + [r for r in reads if r.excl]
            reads = [r for r in reads if not r.excl]
        self._deps(e, reads, writes)
        ins = fn()
        self.cnt[e] += 1
        ins.then_inc(self.sem[e], 1)
        tok = (e, self.sem[e], self.cnt[e])
        self._commit(tok, reads, writes)
        return tok

    def dma(self, q, semname, out, in_, reads=(), writes=(), barrier=True):
        if semname not in self.dsem:
            self.dsem[semname] = self.es.enter_context(self.nc.semaphore("dq_" + semname))
            self.dcnt[semname] = 0
            if not barrier:
                self.nowait_dma.add(semname)
        self._deps(q, reads, writes)
        ins = self.eng[q].dma_start(out=out, in_=in_)
        self.dcnt[semname] += 16
        ins.then_inc(self.dsem[semname], 16)
        tok = ("dma_" + semname, self.dsem[semname], self.dcnt[semname])
        self._commit(tok, reads, writes)
        return tok

    def barrier(self, final=False):
        self.marks.append((self.label, dict(self.cnt)))
        toks = [(e, self.sem[e], self.cnt[e]) for e in ("pe", "act", "dve") if self.cnt[e] > 0]
        for n, s in self.dsem.items():
            if (final or n not in self.nowait_dma) and self.dcnt[n] > 0:
                toks.append(("dma_" + n, s, self.dcnt[n]))
        for e in ("pe", "act", "dve", "sp"):
            for t in toks:
                if t[0] != e:
                    self._wait(e, t)


class Arena:
    def __init__(self, ap_f32, nf32):
        self.ap = ap_f32
        self.n = nf32
        self.off = 0

    def reset(self, off=0):
        self.off = off

    def alloc(self, shape, dt, parts=128):
        nel = 1
        for s in shape:
            nel *= s
        nf = nel if dt == F32 else (nel + 1) // 2
        nf = (nf + 1) // 2 * 2
        assert self.off + nf <= self.n, ("arena overflow", self.off, nf, self.n)
        self.hw = max(getattr(self, "hw", 0), self.off + nf)
        v = self.ap[0:parts, self.off:self.off + nf]
        self.off += nf
        if dt != F32:
            v = v.bitcast(dt)
        v = v[:, 0:nel]
        if len(shape) == 2:
            v = v.rearrange("p (a b) -> p a b", a=shape[0])
        elif len(shape) == 3:
            v = v.rearrange("p (a b c) -> p a b c", a=shape[0], b=shape[1])
        return v


def build_program(tile_sel=None):
    nc = bass.Bass("TRN2", target_bir_lowering=False)
    es = ExitStack()
    dt_in = lambda name, shape: nc.dram_tensor(name, list(shape), F32, kind="ExternalInput").ap()
    dt_out = lambda name, shape: nc.dram_tensor(name, list(shape), F32, kind="ExternalOutput").ap()
    xp = dt_in("xp", (SEQ, D))
    xs = dt_in("xs", (64, D))
    cconv = dt_in("cconv", (2, 3, D))
    cC = dt_in("cC", (2, 4, 256, 256))
    cn = dt_in("cn", (2, 4, 256))
    cm = dt_in("cm", (2, 4))
    cS = dt_in("cS", (2, 8, 128, 128))
    w_in = dt_in("w_in", (D, N_IN))
    w_ba = dt_in("w_ba", (1024, D))
    w_bb = dt_in("w_bb", (1024, D))
    w_out = dt_in("w_out", (D, D))
    w_ff1 = dt_in("w_ff1", (D, 4 * D))
    w_ff2 = dt_in("w_ff2", (4 * D, D))
    gmn_d = dt_in("gmn", (1024,))
    gfin_d = dt_in("gfin", (D,))
    consts_d = dt_in("consts", (128, CW))
    params_d = dt_in("params", (128, PW))
    yp = dt_out("yp", (SEQ, D))
    ys = dt_out("ys", (64, D))
    o_conv_p = dt_out("o_conv_p", (3, D))
    o_C_p = dt_out("o_C_p", (4, 256, 256))
    o_n_p = dt_out("o_n_p", (4, 256))
    o_m_p = dt_out("o_m_p", (4,))
    o_S_p = dt_out("o_S_p", (8, 128, 128))
    o_conv_s = dt_out("o_conv_s", (2, 3, D))
    o_C_s = dt_out("o_C_s", (2, 4, 256, 256))
    o_n_s = dt_out("o_n_s", (2, 4, 256))
    o_m_s = dt_out("o_m_s", (2, 4))
    o_S_s = dt_out("o_S_s", (2, 8, 128, 128))

    es.enter_context(nc.allow_non_contiguous_dma(reason="small state / tail transposing DMAs"))
    sb = lambda name, shape, dt: es.enter_context(nc.sbuf_tensor(name, list(shape), dt))
    wslot = [sb(f"wslot{i}", (128, 16, 512), BF16) for i in range(NSLOT)]
    hT = sb("hT", (128, 16, 512), BF16)
    hmgT = sb("hmgT", (128, 8, 512), BF16)
    ohgT = sb("ohgT", (128, 8, 512), BF16)
    hT_s = sb("hT_s", (128, 16, 64), BF16)
    hmgT_s = sb("hmgT_s", (128, 8, 64), BF16)
    ohgT_s = sb("ohgT_s", (128, 8, 64), BF16)
    GBUF = {"p": dict(hT=hT, hmgT=hmgT, ohgT=ohgT), "s": dict(hT=hT_s, hmgT=hmgT_s, ohgT=ohgT_s)}
    cst = sb("cst", (128, CW), F32)
    par = sb("par", (128, PW), F32)
    gmn_b = sb("gmn_b", (128, 1024), F32)
    ident_b = sb("ident_b", (128, 128), BF16)
    ones_b = sb("ones_b", (128, 128), BF16)
    maskneg_b = sb("maskneg_b", (128, 128), BF16)
    mask4 = sb("mask4", (128, 4, 128), BF16)
    ones_f = sb("ones_f", (128, 128), F32)
    zeros_f = sb("zeros_f", (128, 128), F32)
    wif = sb("wif", (128, 16, 8), BF16)
    lb_t = sb("lb_t", (128, 8), F32)
    oml_t = sb("oml_t", (128, 8), F32)
    noml_t = sb("noml_t", (128, 8), F32)
    Cm = sb("Cm", (128, 4, 2, 256), F32)
    Cbf = sb("Cbf", (128, 4, 2, 256), BF16)
    nm = sb("nm", (128, 4, 2), F32)
    nbf = sb("nbf", (128, 4, 2), BF16)
    Sm = sb("Sm", (128, 8, 128), F32)
    Sbf = sb("Sbf", (128, 8, 128), BF16)
    hist = sb("hist", (128, 16, 4, 3), F32)
    tail = sb("tail", (128, 16, 4, 3), F32)
    m_in = sb("m_in", (4, 8), F32)
    m_fin = sb("m_fin", (4, 8), F32)
    arena_t = sb("arena", (128, ARENA_F32), F32)
    AR = Arena(arena_t[:], ARENA_F32)
    psb = [es.enter_context(nc.psum_tensor(f"psb{i}", [128, 512], F32)) for i in range(8)]
    psbank = [Res(f"psbank{b}", excl=True, bank=b) for b in range(8)]

    def be(prs):
        return "act" if prs[0].bank % 2 == 0 else "dve"

    def PSF(bank, off, n, parts=128):
        return psb[bank][0:parts, off:off + n]

    def PSB(bank, off, n, parts=128):
        return psb[bank][0:parts, :].bitcast(BF16)[:, off:off + n]

    def PR(bank, q0=0, q1=4):
        return [psbank[bank]]

    S = Sched(nc, es)
    R = {}

    def R_hT(T):
        return [RR("hT", T["kind"], kc) for kc in range(16)]

    def R_hmg(T):
        return [RR("hmgT", T["kind"], c) for c in range(8)]

    def R_ohg(T):
        return [RR("ohgT", T["kind"], c) for c in range(8)]

    def RR(*key):
        if key not in R:
            R[key] = Res(str(key))
        return R[key]

    act_e, dve_e, pe_e = nc.scalar, nc.vector, nc.tensor
    ident_f = cst[:, CO_ID:CO_ID + 128]
    maskneg = cst[:, CO_MNEG:CO_MNEG + 128]
    mask01 = cst[:, CO_M01:CO_M01 + 128]
    selm = cst[:, CO_SEL:CO_SEL + 512]
    bmask = cst[:, CO_BM:CO_BM + 16]
    r_cst, r_par = RR("cst"), RR("par")

    unit_specs = []
    ukey = [0]
    wcache = nc.dram_tensor("wcache", [64, 128, 16 * 512], BF16, kind="Internal").ap()

    def add_unit(parts):
        unit_specs.append((ukey[0], parts))
        ukey[0] += 1

    def wv(wd, r0, nr, c0, ncols=512):
        return wd[r0:r0 + nr, c0:c0 + ncols].rearrange("(kc p) n -> p kc n", p=128)

    def front_units():
        ukey[0] = 0
        for (kind, u) in ML_ORDER:
            c0 = {"V": C_V, "OM": C_OM, "K": C_QK + 1024, "Q": C_QK}[kind]
            add_unit([(wv(w_in, 0, D, c0 + u * 512), 0, 16)])
        for c in (C_QH, C_FH, C_IH, C_GH):
            for u in range(2):
                add_unit([(wv(w_in, 0, D, c + u * 512), 0, 16)])

    def back_units():
        ukey[0] = 16
        for q in range(4):
            add_unit([(wv(w_in, 0, D, C_GA + q * 512), 0, 16)])
            add_unit([(wv(w_in, 0, D, C_GB + q * 512), 0, 16)])
            add_unit([(wv(w_ba, 0, 1024, q * 512), 0, 8), (wv(w_bb, 0, 1024, q * 512), 8, 8)])
        for nb in range(4):
            add_unit([(wv(w_out, 0, D, nb * 512), 0, 16)])
        for part in range(NPART):
            for i in range(4):
                add_unit([(wv(w_ff1, 0, D, part * 2048 + i * 512), 0, 16)])
            for nb in range(4):
                add_unit([(wv(w_ff2, part * 2048, 2048, nb * 512), 0, 16)])

    _ids = list(range(5)) if tile_sel is None else sorted(tile_sel)
    for _i in _ids:
        front_units()
        if not (_i == 3 and 4 in _ids):
            back_units()
    ws = {"issued": 0, "next": 0}
    slot_res = [RR("wslot", i) for i in range(NSLOT)]

    n_use, occ = {}, []
    for _i, (_k, _p) in enumerate(unit_specs):
        occ.append(n_use.get(_k, 0))
        n_use[_k] = n_use.get(_k, 0) + 1

    def cache_occ(key):
        c = 0 if key < 16 else (key - 16) % 3
        return min(c, max(n_use[key] - 2, 0))

    def ws_issue(i):
        slot = i % NSLOT
        key, parts = unit_specs[i]
        if occ[i] <= cache_occ(key):
            for (src, kc0, nkc) in parts:
                S.dma("pool", f"w{slot}", wslot[slot][:, kc0:kc0 + nkc, :], src, writes=[slot_res[slot]], barrier=False)
        else:
            S.dma("pool", f"w{slot}", wslot[slot][:].rearrange("p a b -> p (a b)"), wcache[key], reads=[RR("wcache", key)],
                  writes=[slot_res[slot]], barrier=False)

    def ws_next():
        i = ws["next"]
        ws["next"] += 1
        while ws["issued"] < min(len(unit_specs), i + NSLOT):
            ws_issue(ws["issued"])
            ws["issued"] += 1
        slot = i % NSLOT
        key = unit_specs[i][0]
        if occ[i] == cache_occ(key) and n_use[key] > occ[i] + 1:
            S.dma("sp", f"wb{slot}", wcache[key], wslot[slot][:].rearrange("p a b -> p (a b)"), reads=[slot_res[slot]],
                  writes=[RR("wcache", key)], barrier=False)
        return wslot[slot], slot_res[slot]

    S.dma("sp", "cst", cst[:], consts_d, writes=[r_cst])
    S.dma("sp", "par", par[:], params_d, writes=[r_par])
    S.dma("sp", "gmn", gmn_b[:], gmn_d.partition_broadcast(128), writes=[RR("gmn_b")])
    S.dma("pool", "wif", wif[:], w_in[:, C_IF:C_IF + 8].rearrange("(kc p) n -> p kc n", p=128), writes=[RR("wif")])
    S.op("dve", lambda: dve_e.tensor_copy(out=ident_b[:], in_=ident_f), reads=[r_cst], writes=[RR("ident_b")])
    S.op("dve", lambda: dve_e.memset(ones_b[:], 1.0), writes=[RR("ones_b")])
    S.op("dve", lambda: dve_e.tensor_copy(out=maskneg_b[:], in_=maskneg), reads=[r_cst], writes=[RR("maskneg_b")])
    for i4 in range(4):
        S.op("dve", lambda i4=i4: dve_e.tensor_copy(out=mask4[:, i4, :], in_=mask01), reads=[r_cst], writes=[RR("mask4")])
    S.op("dve", lambda: dve_e.memset(ones_f[:], 1.0), writes=[RR("ones_f")])
    S.op("dve", lambda: dve_e.memset(zeros_f[:], 0.0), writes=[RR("zeros_f")])
    r_cpool = [r_cst, r_par, RR("ident_b"), RR("ones_b"), RR("ones_f"), RR("zeros_f"), RR("gmn_b"), RR("wif")]
    S.op("dve", lambda: dve_e.tensor_tensor(out=oml_t[:], in0=par[:, PO_LBL:PO_LBL + 8], in1=par[:, PO_LBL + 8:PO_LBL + 16],
                                            op=ALU.subtract), reads=[r_par], writes=[RR("oml")])
    S.op("act", lambda: act_e.activation(out=lb_t[:], in_=oml_t[:], func=AF.Sigmoid), reads=[RR("oml")], writes=[RR("lb")])
    S.op("dve", lambda: dve_e.tensor_scalar(out=oml_t[:], in0=lb_t[:], scalar1=-1.0, scalar2=1.0, op0=ALU.mult, op1=ALU.add),
         reads=[RR("lb")], writes=[RR("oml")])
    S.op("dve", lambda: dve_e.tensor_scalar(out=noml_t[:], in0=oml_t[:], scalar1=-1.0, scalar2=None, op0=ALU.mult),
         reads=[RR("oml")], writes=[RR("noml")])
    r_cpool += [RR("lb"), RR("oml"), RR("noml")]
    r_st = [RR("state_ml", h) for h in range(4)]
    r_cbf = [[RR("cbf", p, h) for h in range(4)] for p in range(3)]
    r_nbf = [RR("nbf", p) for p in range(3)]
    r_S = [RR("state_hg", j) for j in range(8)]
    r_hist, r_tail, r_min, r_mfin = RR("hist"), RR("tail"), RR("m_in"), RR("m_fin")
    S.op("dve", lambda: dve_e.memset(Cm[:], 0.0), writes=r_st)
    S.op("dve", lambda: dve_e.memset(nm[:], 0.0), writes=r_st)
    S.op("dve", lambda: dve_e.memset(Cbf[:], 0.0), writes=r_cbf[0])
    S.op("dve", lambda: dve_e.memset(nbf[:], 0.0), writes=[r_nbf[0]])
    S.op("dve", lambda: dve_e.memset(Sm[:], 0.0), writes=r_S)
    S.op("dve", lambda: dve_e.memset(Sbf[:], 0.0), writes=r_S)
    S.op("dve", lambda: dve_e.memset(hist[:], 0.0), writes=[r_hist])
    S.op("dve", lambda: dve_e.memset(m_in[:], 0.0), writes=[r_min])
    S.barrier()

    rr_flip = {"i": 0}

    def alt():
        rr_flip["i"] ^= 1
        return "act" if rr_flip["i"] else "dve"

    def copy_op(e, out, in_, reads, writes):
        if e == "act":
            return S.op("act", lambda: act_e.copy(out=out, in_=in_), reads=reads, writes=writes)
        return S.op("dve", lambda: dve_e.tensor_copy(out=out, in_=in_), reads=reads, writes=writes)

    def scale_op(e, out, in_, sc_ap, reads, writes):
        if e == "act":
            return S.op("act", lambda: act_e.activation(out=out, in_=in_, func=AF.Copy, scale=sc_ap), reads=reads, writes=writes)
        return S.op("dve", lambda: dve_e.tensor_scalar(out=out, in0=in_, scalar1=sc_ap, scalar2=None, op0=ALU.mult),
                    reads=reads, writes=writes)

    def mm_group(out, pairs, reads, writes, start=True, stop=True):
        def fn():
            ins = None
            n = len(pairs)
            for i, (l, r) in enumerate(pairs):
                ins = pe_e.matmul(out, lhsT=l, rhs=r, start=(start and i == 0), stop=(stop and i == n - 1))
            return ins
        return S.op("pe", fn, reads=reads, writes=writes)

    def transp(out, in_, ident, reads, writes):
        return S.op("pe", lambda: pe_e.transpose(out=out, in_=in_, identity=ident), reads=reads, writes=writes)

    def rmsnorm_to_FM(T, x_tm, r_x, gcol, dstT, r_dst, keep=False):
        LB, NB = T["LB"], T["NB"]
        off0 = AR.off
        xn = [AR.alloc((2048,), BF16) for _ in range(2)]
        ssq = AR.alloc((8,), F32)
        for b in range(NB):
            xb = x_tm[0:LB, b, :]
            rs, rx = RR("n_ssq", T["id"], id(dstT), b), RR("n_xn", T["id"], id(dstT), b % 2)
            S.op("act", lambda: act_e.activation(out=xn[b % 2][0:LB, :], in_=xb, func=AF.Square, accum_out=ssq[0:LB, b:b + 1]),
                 reads=[r_x[b]], writes=[rx, rs])
            S.op("act", lambda: act_e.activation(out=ssq[0:LB, b:b + 1], in_=ssq[0:LB, b:b + 1], func=AF.Ln,
                                                 scale=1.0 / D, bias=eps_t[0:LB, :]), reads=[rs], writes=[rs])
            S.op("act", lambda: act_e.activation(out=ssq[0:LB, b:b + 1], in_=ssq[0:LB, b:b + 1], func=AF.Exp, scale=-0.5), reads=[rs], writes=[rs])
            xnb = xn[b % 2]
            scale_op("dve", xnb[0:LB, :], xb, ssq[0:LB, b:b + 1], [r_x[b], rs], [rx])
            for half in range(2):
                bank = (b % 4) * 2 + half
                for c8 in range(8):
                    kc = half * 8 + c8
                    transp(PSB(bank, c8 * 128, LB), xnb[0:LB, kc * 128:(kc + 1) * 128], ident_b[0:LB, 0:LB],
                           [rx] + r_cpool[2:3], PR(bank, c8 // 2, c8 // 2 + 1))
                gb = par[:, gcol + half * 8:gcol + half * 8 + 8].unsqueeze(2).broadcast_to([128, 8, LB])
                S.op("dve", lambda: dve_e.tensor_tensor(out=dstT[:, half * 8:half * 8 + 8, b * LB:(b + 1) * LB],
                                                        in0=PSB(bank, 0, 1024).rearrange("p (a c) -> p a c", a=8)[:, :, 0:LB], in1=gb, op=ALU.mult),
                     reads=PR(bank) + [r_par], writes=r_dst[half * 8:half * 8 + 8])
        if not keep:
            AR.reset(off0)

    eps_t = sb("eps_t", (128, 1), F32)
    S.op("dve", lambda: dve_e.memset(eps_t[:], EPS), writes=[RR("eps")])
    S.barrier()

    def proj_TM(T, wt, r_w, srcT, r_src, nkc, bankset, evac):
        LB, NB = T["LB"], T["NB"]
        for b in range(NB):
            bank = bankset * 4 + b
            mm_group(PSF(bank, 0, 512, LB), [(srcT[:, kc, b * LB:(b + 1) * LB], wt[:, kc, :]) for kc in range(nkc)],
                     [r_w] + r_src, PR(bank))
            evac(b, PSF(bank, 0, 512, LB), PR(bank))

    def proj_FM(T, wt, r_w, srcT, r_src, kcs, bankset, evac, wkc0=0):
        TT = T["TT"]
        for j in range(4):
            bank = bankset * 4 + j
            mm_group(PSF(bank, 0, TT), [(wt[:, wkc0 + i, j * 128:(j + 1) * 128], srcT[:, kc, 0:TT]) for i, kc in enumerate(kcs)],
                     [r_w] + r_src, PR(bank))
            evac(j, PSF(bank, 0, TT), PR(bank))

    def phase_mlstm(T):
        LB, NB, TT, tid = T["LB"], T["NB"], T["TT"], T["id"]
        AR.reset(0)
        v_tm = AR.alloc((NB, 1024), BF16)
        so_tm = AR.alloc((NB, 1024), BF16)
        kT = AR.alloc((8, TT), BF16)
        qT = AR.alloc((8, TT), BF16)
        r_v = [RR("v_tm", tid, b) for b in range(NB)]
        r_so = [RR("so_tm", tid, b) for b in range(NB)]
        r_kT = [RR("kT", tid, c) for c in range(8)]
        r_qT = [RR("qT", tid, c) for c in range(8)]
        r_hT = R_hT(T)
        hT = GBUF[T["kind"]]["hT"]
        hmgT = GBUF[T["kind"]]["hmgT"]
        bs = [0]

        def nbs():
            bs[0] ^= 1
            return bs[0]
        off_tmp = AR.off
        sg_tmp = [AR.alloc((512,), F32) for _ in range(2)]
        k = [0]
        qkpad = [AR.alloc((NB, LB + 3), F32) for _ in range(2)]
        accb = [AR.alloc((NB, LB), F32) for _ in range(2)]
        sgb = [AR.alloc((NB, LB), F32) for _ in range(2)]
        cc = [0]
        for (kind, u) in ML_ORDER:
            wt, r_w = ws_next()
            if kind == "V":
                def ev(b, ps, prs, u=u):
                    copy_op(be(prs), v_tm[0:LB, b, u * 512:(u + 1) * 512], ps, prs, [r_v[b]])
                proj_TM(T, wt, r_w, hT, r_hT, 16, nbs(), ev)
            elif kind == "OM":
                def ev(b, ps, prs, u=u):
                    k[0] ^= 1
                    t, rt = sg_tmp[k[0]], RR("sg_tmp", tid, k[0])
                    S.op("act", lambda: act_e.activation(out=t[0:LB, :], in_=ps, func=AF.Sigmoid), reads=prs, writes=[rt])
                    S.op("dve", lambda: dve_e.tensor_tensor(out=so_tm[0:LB, b, u * 512:(u + 1) * 512], in0=t[0:LB, :],
                                                            in1=gmn_b[0:LB, u * 512:(u + 1) * 512], op=ALU.mult),
                         reads=[rt, RR("gmn_b")], writes=[r_so[b]])
                proj_TM(T, wt, r_w, hT, r_hT, 16, nbs(), ev)
            else:
                isq = 1 if kind == "Q" else 0

                def ev(j, ps, prs, u=u, isq=isq):
                    cc[0] ^= 1
                    i = cc[0]
                    pad, acc, sg = qkpad[i], accb[i], sgb[i]
                    rp, ra, rs = RR("qkpad", tid, i), RR("accb", tid, i), RR("sgb", tid, i)
                    c8 = u * 4 + j
                    c16 = c8 if isq else 8 + c8
                    wc = lambda ii: par[:, PO_WCONV + c16 * 4 + ii:PO_WCONV + c16 * 4 + ii + 1]
                    S.op("act", lambda: act_e.copy(out=pad[:, :, 3:3 + LB], in_=ps.rearrange("p (a b) -> p a b", a=NB)),
                         reads=prs, writes=[rp])
                    S.op("dve", lambda: dve_e.tensor_copy(out=pad[:, :, 0:3], in_=hist[:, c16, 0:NB, :]), reads=[r_hist], writes=[rp])
                    if T["kind"] == "p":
                        S.op("dve", lambda: dve_e.tensor_copy(out=pad[:, 1:NB, 0:3], in_=pad[:, 0:NB - 1, LB:LB + 3]),
                             reads=[rp], writes=[rp])
                        S.op("dve", lambda: dve_e.tensor_copy(out=hist[:, c16, 0, :], in_=pad[:, NB - 1, LB:LB + 3]),
                             reads=[rp], writes=[r_hist])
                    S.op("dve", lambda: dve_e.tensor_copy(out=tail[:, c16, 0:NB, :], in_=pad[:, :, LB:LB + 3]), reads=[rp], writes=[r_tail])
                    S.op("act", lambda: act_e.activation(out=acc[:], in_=pad[:, :, 3:3 + LB], func=AF.Identity, scale=wc(3),
                                                         bias=par[:, PO_BCONV + c16:PO_BCONV + c16 + 1]),
                         reads=[rp, r_par], writes=[ra])
                    for ii in (2, 1, 0):
                        S.op("dve", lambda ii=ii: dve_e.scalar_tensor_tensor(out=acc[:], in0=pad[:, :, ii:ii + LB], scalar=wc(ii), in1=acc[:],
                                                                             op0=ALU.mult, op1=ALU.add),
                             reads=[rp, ra, r_par], writes=[ra])
                    S.op("act", lambda: act_e.activation(out=sg[:], in_=acc[:], func=AF.Sigmoid), reads=[ra], writes=[rs])
                    dst = (qT if isq else kT)[:, c8, :].rearrange("p (a b) -> p a b", a=NB)
                    rdst = (r_qT if isq else r_kT)[c8]
                    scl = 1.0 if isq else 1.0 / 16.0
                    S.op("dve", lambda: dve_e.scalar_tensor_tensor(out=dst, in0=acc[:], scalar=scl, in1=sg[:], op0=ALU.mult, op1=ALU.mult),
                         reads=[ra, rs], writes=[rdst])
                proj_FM(T, wt, r_w, hT, r_hT, list(range(16)), nbs(), ev)
        gsb = AR.alloc((2, TT), F32, parts=4)
        r_g = RR("gsb", tid)
        for gi in range(2):
            mm_group(PSF(gi, 0, TT, 4), [(wif[:, kc, gi * 4:gi * 4 + 4], hT[:, kc, 0:TT]) for kc in range(16)],
                     [RR("wif")] + r_hT, PR(gi))
            S.op("act", lambda gi=gi: act_e.activation(out=gsb[0:4, gi, :], in_=PSF(gi, 0, TT, 4), func=AF.Identity,
                                                       bias=par[0:4, PO_BIF + gi:PO_BIF + gi + 1]), reads=PR(gi) + [r_par], writes=[r_g])
        lf = AR.alloc((TT,), F32, parts=4)
        t1 = AR.alloc((TT,), F32, parts=4)
        r_lf, r_t1 = RR("lf", tid), RR("t1", tid)
        S.op("act", lambda: act_e.activation(out=t1[0:4, :], in_=gsb[0:4, 1, :], func=AF.Abs), reads=[r_g], writes=[r_t1])
        S.op("act", lambda: act_e.activation(out=t1[0:4, :], in_=t1[0:4, :], func=AF.Exp, scale=-1.0), reads=[r_t1], writes=[r_t1])
        S.op("act", lambda: act_e.activation(out=t1[0:4, :], in_=t1[0:4, :], func=AF.Ln, bias=ones_f[0:4, 0:1]), reads=[r_t1, RR("ones_f")], writes=[r_t1])
        S.op("dve", lambda: dve_e.tensor_single_scalar(out=lf[0:4, :], in_=gsb[0:4, 1, :], scalar=0.0, op=ALU.min), reads=[r_g], writes=[r_lf])
        S.op("dve", lambda: dve_e.tensor_tensor(out=lf[0:4, :], in0=lf[0:4, :], in1=t1[0:4, :], op=ALU.subtract), reads=[r_lf, r_t1], writes=[r_lf])
        rows2 = [AR.alloc((8, LB), F32, parts=4) for _ in range(2)]
        rexp = AR.alloc((16,), F32, parts=4)
        gtm2 = [AR.alloc((16,), F32) for _ in range(2)]
        decb2 = [AR.alloc((4,), F32) for _ in range(2)]
        CB = [Cbf, AR.alloc((4, 2, 256), BF16), AR.alloc((4, 2, 256), BF16)]
        NBF = [nbf, AR.alloc((4, 2), BF16), AR.alloc((4, 2), BF16)]
        Dt = AR.alloc((4, LB), F32)
        Pt = AR.alloc((4, LB), BF16)
        intra_sb = AR.alloc((4, 256), F32)
        hnum = AR.alloc((4, 256), F32)
        sqj = AR.alloc((256,), BF16)
        hm_tm = AR.alloc((1024,), BF16)
        kw = AR.alloc((4, 256), BF16)
        den_sb = AR.alloc((8,), F32)
        sm = AR.alloc((8, 4), F32)
        trs = AR.alloc((4, 2, 256), F32) if T["kind"] == "s" else None
        r_rexp = RR("rexp", tid)
        r_rows2 = [RR("rows", tid, p) for p in range(2)]
        r_gtm2 = [RR("gtm", tid, p) for p in range(2)]
        r_decb2 = [RR("decb", tid, p) for p in range(2)]
        r_den, r_sm = RR("den_sb", tid), RR("sm", tid)
        r_Dt = [RR("Dt", tid, h) for h in range(4)]
        r_Pt = [RR("Pt", tid, h) for h in range(4)]
        r_is = [RR("intra_sb", tid, h) for h in range(4)]
        r_hn = [RR("hnum", tid, h) for h in range(4)]
        r_kw = [RR("kw", tid, h) for h in range(4)]
        r_hm = [RR("hm_tm", tid, h) for h in range(4)]
        r_sqj = RR("sqj", tid)
        r_hmg = R_hmg(T)

        def upd(b):
            cs = slice(b * LB, (b + 1) * LB)
            p = b % 2
            rows, gtm, decb = rows2[p], gtm2[p], decb2[p]
            r_rows, r_gtm, r_decb = r_rows2[p], r_gtm2[p], r_decb2[p]
            if T["kind"] == "s":
                load_state_ml(T, b, trs, CB[b % 3], r_cbf[b % 3], NBF[b % 3], r_nbf[b % 3])
            S.op("dve", lambda: dve_e.tensor_tensor_scan(out=rows[0:4, 0, :], data0=lf[0:4, cs], data1=zeros_f[0:4, 0:LB], initial=0.0,
                                                         op0=ALU.add, op1=ALU.add), reads=[r_lf, RR("zeros_f")], writes=[r_rows])
            S.op("dve", lambda: dve_e.tensor_tensor(out=rows[0:4, 1, :], in0=gsb[0:4, 0, cs], in1=rows[0:4, 0, :], op=ALU.subtract),
                 reads=[r_g, r_rows], writes=[r_rows])
            S.op("dve", lambda: dve_e.tensor_tensor_scan(out=rows[0:4, 2, :], data0=rows[0:4, 1, :], data1=rows[0:4, 1, :],
                                                         initial=m_in[0:4, b:b + 1], op0=ALU.max, op1=ALU.max),
                 reads=[r_rows, r_min], writes=[r_rows])
            S.op("dve", lambda: dve_e.tensor_scalar(out=rows[0:4, 3, :], in0=rows[0:4, 2, :], scalar1=-1.0, scalar2=None, op0=ALU.mult),
                 reads=[r_rows], writes=[r_rows])
            S.op("act", lambda: act_e.activation(out=rows[0:4, 4, :], in_=rows[0:4, 2, :], func=AF.Exp, scale=-1.0, bias=m_in[0:4, b:b + 1]),
                 reads=[r_rows, r_min], writes=[r_rows])
            S.op("dve", lambda: dve_e.tensor_tensor(out=rows[0:4, 7, :], in0=rows[0:4, 0, :], in1=rows[0:4, 2, :], op=ALU.add),
                 reads=[r_rows], writes=[r_rows])
            S.op("act", lambda: act_e.activation(out=rows[0:4, 5, :], in_=rows[0:4, 7, :], func=AF.Exp, scale=-1.0), reads=[r_rows], writes=[r_rows])
            S.op("act", lambda: act_e.activation(out=rows[0:4, 6, :], in_=rows[0:4, 1, :], func=AF.Exp, bias=rows[0:4, 3, LB - 1:LB]),
                 reads=[r_rows], writes=[r_rows])
            if T["kind"] == "p":
                if b + 1 < NB:
                    S.op("dve", lambda: dve_e.tensor_copy(out=m_in[0:4, b + 1:b + 2], in_=rows[0:4, 7, LB - 1:LB]), reads=[r_rows], writes=[r_min])
                else:
                    S.op("dve", lambda: dve_e.tensor_copy(out=m_fin[0:4, 0:1], in_=rows[0:4, 7, LB - 1:LB]), reads=[r_rows], writes=[r_mfin])
                    S.op("dve", lambda: dve_e.tensor_copy(out=m_in[0:4, 0:1], in_=rows[0:4, 7, LB - 1:LB]), reads=[r_rows], writes=[r_min])
            else:
                S.op("dve", lambda: dve_e.tensor_copy(out=m_fin[0:4, b:b + 1], in_=rows[0:4, 7, LB - 1:LB]), reads=[r_rows], writes=[r_mfin])
            S.op("dve", lambda: dve_e.tensor_scalar(out=rexp[0:4, 0:4], in0=bmask[0:4, 0:4], scalar1=rows[0:4, 4, LB - 1:LB], scalar2=None,
                                                    op0=ALU.mult), reads=[r_rows, r_cst], writes=[r_rexp])
            mm_group(PSF(6, 0, 4), [(ones_f[0:4, 0:128], rexp[0:4, 0:4])], [RR("ones_f"), r_rexp], PR(6, 0, 1))
            for i, ri in enumerate((1, 6, 4, 5)):
                transp(PSF(6, 16 + i * 4, 4, LB), rows[0:4, ri, :], ident_f[0:4, 0:4], [r_rows, r_cst], PR(6, 0, 1))
            S.op("dve", lambda: dve_e.tensor_copy(out=decb[:, 0:4], in_=PSF(6, 0, 4)), reads=PR(6, 0, 1), writes=[r_decb])
            S.op("act", lambda: act_e.copy(out=gtm[0:LB, 0:16], in_=PSF(6, 16, 16, LB)), reads=PR(6, 0, 1), writes=[r_gtm])
            for h in range(4):
                for kc in range(2):
                    transp(PSB(7, h * 256 + kc * 128, 128, LB), kT[:, 2 * h + kc, cs], ident_b[:, :], [r_kT[2 * h + kc], RR("ident_b")], PR(7, h, h + 1))
            S.op("dve", lambda: dve_e.tensor_tensor(out=kw[0:LB, :, :], in0=PSB(7, 0, 1024, LB).rearrange("p (a c) -> p a c", a=4),
                                                    in1=gtm[0:LB, 4:8].unsqueeze(2).broadcast_to([LB, 4, 256]), op=ALU.mult),
                 reads=PR(7) + [r_gtm], writes=r_kw)
            for h in range(4):
                bank = 2 + h
                for kc in range(2):
                    mm_group(PSF(bank, kc * 256, 256), [(kw[0:LB, h, kc * 128:(kc + 1) * 128], v_tm[0:LB, b, h * 256:(h + 1) * 256])],
                             [r_kw[h], r_v[b]], PR(bank, kc * 2, kc * 2 + 2))
                    mm_group(PSF(6, 256 + 2 * h + kc, 1), [(kw[0:LB, h, kc * 128:(kc + 1) * 128], ones_b[0:LB, 0:1])], [r_kw[h], RR("ones_b")], PR(6, 2, 3))
            for h in range(4):
                bank = 2 + h
                S.op("dve", lambda h=h, bank=bank: dve_e.scalar_tensor_tensor(
                    out=Cm[:, h, :, :], in0=Cm[:, h, :, :], scalar=decb[:, h:h + 1], in1=PSF(bank, 0, 512).rearrange("p (a b) -> p a b", a=2),
                    op0=ALU.mult, op1=ALU.add), reads=[r_st[h], r_decb] + PR(bank), writes=[r_st[h]])
                S.op("dve", lambda h=h: dve_e.scalar_tensor_tensor(out=nm[:, h, :], in0=nm[:, h, :], scalar=decb[:, h:h + 1], in1=PSF(6, 256 + 2 * h, 2),
                                                                   op0=ALU.mult, op1=ALU.add), reads=[r_st[h], r_decb] + PR(6, 2, 3), writes=[r_st[h]])
            if T["kind"] == "p":
                if b + 1 < NB:
                    q = (b + 1) % 3
                    for h in range(4):
                        S.op("act", lambda h=h: act_e.copy(out=CB[q][:, h, :, :], in_=Cm[:, h, :, :]), reads=[r_st[h]], writes=[r_cbf[q][h]])
                    S.op("act", lambda: act_e.copy(out=NBF[q][:], in_=nm[:]), reads=r_st, writes=[r_nbf[q]])
            else:
                store_state_ml(T, b, trs, o_C_s[b], o_n_s[b])

        def rd(b):
            cs = slice(b * LB, (b + 1) * LB)
            p = b % 2
            rows, gtm = rows2[p], gtm2[p]
            r_rows, r_gtm = r_rows2[p], r_gtm2[p]
            p3 = b % 3
            Cb, nb_ = CB[p3], NBF[p3]
            for h in range(4):
                mm_group(PSF(0, h * 128, LB, LB), [(kT[:, 2 * h + kc, cs], qT[:, 2 * h + kc, cs]) for kc in range(2)],
                         [r_kT[2 * h], r_kT[2 * h + 1], r_qT[2 * h], r_qT[2 * h + 1]], PR(0, h, h + 1))
            for h in range(4):
                mm_group(PSF(1, h * 128, LB, LB), [(selm[0:4, h * 128:h * 128 + LB], rows[0:4, 3, :]), (ident_b[0:LB, 0:LB], maskneg_b[0:LB, 0:LB])],
                         [r_cst, r_rows, RR("ident_b"), RR("maskneg_b")], PR(1, h, h + 1))
            for h in range(4):
                S.op("act", lambda h=h: act_e.activation(out=Dt[0:LB, h, :], in_=PSF(1, h * 128, LB, LB), func=AF.Exp, bias=gtm[0:LB, h:h + 1]),
                     reads=PR(1, h, h + 1) + [r_gtm], writes=[r_Dt[h]])
            S.op("dve", lambda: dve_e.tensor_tensor(out=Pt[0:LB, :, :], in0=PSF(0, 0, 512, LB).rearrange("p (a c) -> p a c", a=4)[:, :, 0:LB],
                                                    in1=Dt[0:LB, :, :], op=ALU.mult), reads=PR(0) + r_Dt, writes=r_Pt)
            for h in range(4):
                bank, off = 4 + h // 2, (h % 2) * 256
                mm_group(PSF(bank, off, 256, LB), [(Pt[0:LB, h, :], v_tm[0:LB, b, h * 256:(h + 1) * 256])], [r_Pt[h], r_v[b]],
                         PR(bank, (h % 2) * 2, (h % 2) * 2 + 2))
                mm_group(PSF(6, 128 + 2 * h + 1, 1, LB), [(Pt[0:LB, h, :], ones_b[0:LB, 0:1])], [r_Pt[h], RR("ones_b")], PR(6, 1, 2))
            for h in range(4):
                bank, off = 2 + h // 2, (h % 2) * 256
                mm_group(PSF(bank, off, 256, LB), [(qT[:, 2 * h + kc, cs], Cb[:, h, kc, :]) for kc in range(2)],
                         [r_qT[2 * h], r_qT[2 * h + 1], r_cbf[p3][h]], PR(bank, (h % 2) * 2, (h % 2) * 2 + 2))
                mm_group(PSF(6, 128 + 2 * h, 1, LB), [(qT[:, 2 * h + kc, cs], nb_[:, h, kc:kc + 1]) for kc in range(2)],
                         [r_qT[2 * h], r_qT[2 * h + 1], r_nbf[p3]], PR(6, 1, 2))
            for i2 in range(2):
                S.op("act", lambda i2=i2: act_e.copy(out=intra_sb[0:LB, 2 * i2:2 * i2 + 2, :], in_=PSF(4 + i2, 0, 512, LB).rearrange("p (a c) -> p a c", a=2)),
                     reads=PR(4 + i2), writes=r_is[2 * i2:2 * i2 + 2])
            S.op("dve", lambda: dve_e.tensor_copy(out=den_sb[0:LB, 0:8], in_=PSF(6, 128, 8, LB)), reads=PR(6, 1, 2), writes=[r_den])
            for h in range(4):
                bank, off = 2 + h // 2, (h % 2) * 256
                S.op("dve", lambda h=h, bank=bank, off=off: dve_e.scalar_tensor_tensor(
                    out=hnum[0:LB, h, :], in0=PSF(bank, off, 256, LB), scalar=gtm[0:LB, 8 + h:9 + h], in1=intra_sb[0:LB, h, :],
                    op0=ALU.mult, op1=ALU.add), reads=PR(bank, (h % 2) * 2, (h % 2) * 2 + 2) + [r_gtm, r_is[h]], writes=[r_hn[h]])
            dnq = den_sb[0:LB, 0:8].rearrange("p (h t) -> p h t", t=2)
            S.op("dve", lambda: dve_e.tensor_tensor(out=sm[0:LB, 0, :], in0=dnq[:, :, 0], in1=gtm[0:LB, 8:12], op=ALU.mult), reads=[r_den, r_gtm], writes=[r_sm])
            S.op("dve", lambda: dve_e.tensor_tensor(out=sm[0:LB, 0, :], in0=sm[0:LB, 0, :], in1=dnq[:, :, 1], op=ALU.add), reads=[r_den, r_sm], writes=[r_sm])
            S.op("act", lambda: act_e.activation(out=sm[0:LB, 0, :], in_=sm[0:LB, 0, :], func=AF.Abs), reads=[r_sm], writes=[r_sm])
            S.op("dve", lambda: dve_e.tensor_tensor(out=sm[0:LB, 0, :], in0=sm[0:LB, 0, :], in1=gtm[0:LB, 12:16], op=ALU.max), reads=[r_sm, r_gtm], writes=[r_sm])
            S.op("dve", lambda: dve_e.reciprocal(out=sm[0:LB, 1, :], in_=sm[0:LB, 0, :]), reads=[r_sm], writes=[r_sm])
            for h in range(4):
                S.op("act", lambda h=h: act_e.activation(out=sqj[0:LB, :], in_=hnum[0:LB, h, :], func=AF.Square, accum_out=sm[0:LB, 2, h:h + 1]),
                     reads=[r_hn[h]], writes=[r_sqj, r_sm])
            S.op("dve", lambda: dve_e.tensor_tensor(out=sm[0:LB, 3, :], in0=sm[0:LB, 1, :], in1=sm[0:LB, 1, :], op=ALU.mult), reads=[r_sm], writes=[r_sm])
            S.op("dve", lambda: dve_e.tensor_tensor(out=sm[0:LB, 3, :], in0=sm[0:LB, 3, :], in1=sm[0:LB, 2, :], op=ALU.mult), reads=[r_sm], writes=[r_sm])
            S.op("act", lambda: act_e.activation(out=sm[0:LB, 3, :], in_=sm[0:LB, 3, :], func=AF.Ln, scale=1.0 / 256.0, bias=eps_t[0:LB, :]),
                 reads=[r_sm, RR("eps")], writes=[r_sm])
            S.op("act", lambda: act_e.activation(out=sm[0:LB, 4, :], in_=sm[0:LB, 3, :], func=AF.Exp, scale=-0.5), reads=[r_sm], writes=[r_sm])
            S.op("dve", lambda: dve_e.tensor_tensor(out=sm[0:LB, 5, :], in0=sm[0:LB, 4, :], in1=sm[0:LB, 1, :], op=ALU.mult), reads=[r_sm], writes=[r_sm])
            for h in range(4):
                S.op("dve", lambda h=h: dve_e.scalar_tensor_tensor(out=hm_tm[0:LB, h * 256:(h + 1) * 256], in0=hnum[0:LB, h, :], scalar=sm[0:LB, 5, h:h + 1],
                                                                   in1=so_tm[0:LB, b, h * 256:(h + 1) * 256], op0=ALU.mult, op1=ALU.mult),
                     reads=[r_hn[h], r_sm, r_so[b]], writes=[r_hm[h]])
            for h in range(4):
                for kc in range(2):
                    transp(PSB(1, h * 256 + kc * 128, LB), hm_tm[0:LB, h * 256 + kc * 128:h * 256 + (kc + 1) * 128], ident_b[0:LB, 0:LB],
                           [r_hm[h], RR("ident_b")], PR(1, h, h + 1))
            copy_op(be(PR(1)), hmgT[:, 0:8, cs], PSB(1, 0, 1024).rearrange("p (a b) -> p a b", a=8)[:, :, 0:LB], PR(1), r_hmg)

        upd(0)
        for b in range(NB):
            if b + 1 < NB:
                upd(b + 1)
            rd(b)
        if T["kind"] == "p":
            for h in range(4):
                S.op("act", lambda h=h: act_e.copy(out=Cbf[:, h, :, :], in_=Cm[:, h, :, :]), reads=[r_st[h]], writes=[r_cbf[0][h]])
            S.op("act", lambda: act_e.copy(out=nbf[:], in_=nm[:]), reads=r_st, writes=[r_nbf[0]])
        if T["kind"] == "p" and T["last"]:
            S.barrier()
            AR.reset(off_tmp)
            trs = AR.alloc((4, 2, 256), F32)
            store_state_ml(T, 0, trs, o_C_p, o_n_p)
        S.barrier()

    def load_state_ml(T, b, trs, Cb, r_cb, nb_, r_nb):
        tid = T["id"]
        r_trs = RR("trs", tid)
        S.dma("sp", "trs", trs[:].rearrange("p h vc k -> p (h vc) k"), cC[b].rearrange("h (vc p) k -> p (h vc) k", p=128), writes=[r_trs])
        for h in range(4):
            for vc in range(2):
                for kc in range(2):
                    transp(PSF(7, (vc * 2 + kc) * 128, 128), trs[:, h, vc, kc * 128:(kc + 1) * 128], ident_f, [r_trs, r_cst], PR(7))
            for vc in range(2):
                copy_op(be(PR(7)), Cm[:, h, :, vc * 128:(vc + 1) * 128], PSF(7, vc * 256, 256).rearrange("p (kc v) -> p kc v", kc=2), PR(7), [r_st[h]])
            S.op("act", lambda h=h: act_e.copy(out=Cb[:, h, :, :], in_=Cm[:, h, :, :]), reads=[r_st[h]], writes=[r_cb[h]])
        for h in range(4):
            S.dma("sp", f"nm_in{h}", nm[:, h, :], cn[b, h].rearrange("(kc p) -> p kc", p=128), writes=[r_st[h]])
        S.op("act", lambda: act_e.copy(out=nb_[:], in_=nm[:]), reads=r_st, writes=[r_nb])
        S.dma("sp", "m_in", m_in[0:4, b:b + 1], cm[b:b + 1, :].rearrange("a h -> h a"), writes=[r_min])

    def store_state_ml(T, b, trs, oC, on):
        tid = T["id"]
        r_trs = RR("trs", tid)
        for h in range(4):
            for vc in range(2):
                for kc in range(2):
                    transp(PSF(7, (vc * 2 + kc) * 128, 128), Cm[:, h, kc, vc * 128:(vc + 1) * 128], ident_f, [r_st[h], r_cst], PR(7))
            copy_op(be(PR(7)), trs[:, h, :, :], PSF(7, 0, 512).rearrange("p (vc k) -> p vc k", vc=2), PR(7), [r_trs])
        S.dma("sp", "trs", oC.rearrange("h (vc p) k -> p (h vc) k", p=128), trs[:].rearrange("p h vc k -> p (h vc) k"), reads=[r_trs])
        for h in range(4):
            S.dma("sp", f"nm_out{h}", on[h].rearrange("(kc p) -> p kc", p=128), nm[:, h, :], reads=[r_st[h]])

    def phase_hgrn(T):
        LB, NB, TT, tid = T["LB"], T["NB"], T["TT"], T["id"]
        AR.reset(0)
        i_tm = AR.alloc((NB, 1024), BF16)
        qhT = AR.alloc((8, TT), BF16)
        off_prods = AR.off
        qt_ = AR.alloc((8, TT), BF16)
        kt_ = AR.alloc((8, TT), BF16)
        qh_ = AR.alloc((8, TT), BF16)
        kh_ = AR.alloc((8, TT), BF16)
        aref = AR.alloc((8, NB, 4), F32)
        off_prep = AR.off
        fbs = [AR.alloc((3, TT), F32) for _ in range(4)]
        ebuf = [AR.alloc((NB, LB), F32) for _ in range(4)]
        r_i = [RR("i_tm", tid, b) for b in range(NB)]
        r_qh = [RR("qhT", tid, j) for j in range(8)]
        r_gs = [RR("gsT", tid, j) for j in range(8)]
        r_o = [RR("o_all", tid, j) for j in range(8)]
        r_hT = R_hT(T)
        hT = GBUF[T["kind"]]["hT"]
        ohgT = GBUF[T["kind"]]["ohgT"]
        bs = [0]

        def nbs():
            bs[0] ^= 1
            return bs[0]
        MID = LB // 2 - 1
        rp = [[RR("hgprod", tid, j, a) for a in range(4)] for j in range(8)]
        r_prod = [None] * 8
        r_aref = [RR("aref", tid, j) for j in range(8)]
        for u in range(2):
            wt, r_w = ws_next()

            def ev(j, ps, prs, u=u):
                copy_op(be(prs), qhT[:, u * 4 + j, :], ps, prs, [r_qh[u * 4 + j]])
            proj_FM(T, wt, r_w, hT, r_hT, list(range(16)), nbs(), ev)

        def fh_unit(u):
            wt, r_w = ws_next()

            def ev(jj, ps, prs, u=u):
                S.op("act", lambda: act_e.activation(out=fbs[jj][:, 0, :], in_=ps, func=AF.Sigmoid), reads=prs, writes=[RR("fbuf", tid, jj, 0)])
            proj_FM(T, wt, r_w, hT, r_hT, list(range(16)), nbs(), ev)

        def prep(u):
            H = range(4)
            J = [4 * u + jj for jj in H]
            rf = [[RR("fbuf", tid, jj, i) for i in range(3)] for jj in H]
            re_ = [RR("ebuf", tid, jj) for jj in H]
            a3 = [fbs[jj][:, 2, :].rearrange("p (a b) -> p a b", a=NB) for jj in H]
            w3 = [fbs[jj][:, 0, :].rearrange("p (a b) -> p a b", a=NB) for jj in H]
            k3 = [fbs[jj][:, 1, :].rearrange("p (a b) -> p a b", a=NB) for jj in H]
            q3 = [qhT[:, J[jj], :].rearrange("p (a b) -> p a b", a=NB) for jj in H]
            for jj in H:
                j = J[jj]
                S.op("dve", lambda jj=jj, j=j: dve_e.tensor_scalar(out=fbs[jj][:, 1, :], in0=fbs[jj][:, 0, :], scalar1=noml_t[:, j:j + 1], scalar2=oml_t[:, j:j + 1],
                                                                   op0=ALU.mult, op1=ALU.add), reads=[rf[jj][0], RR("noml"), RR("oml")], writes=[rf[jj][1]])
            for jj in H:
                j = J[jj]
                S.op("act", lambda jj=jj, j=j: act_e.activation(out=fbs[jj][:, 0, :], in_=fbs[jj][:, 0, :], func=AF.Ln, scale=oml_t[:, j:j + 1], bias=lb_t[:, j:j + 1]),
                     reads=[rf[jj][0], RR("oml"), RR("lb")], writes=[rf[jj][0]])
            for jj in H:
                for b in range(NB):
                    cs = slice(b * LB, (b + 1) * LB)
                    S.op("dve", lambda jj=jj, cs=cs: dve_e.tensor_tensor_scan(out=fbs[jj][:, 2, cs], data0=fbs[jj][:, 0, cs], data1=zeros_f[:, 0:LB], initial=0.0,
                                                                              op0=ALU.add, op1=ALU.add), reads=[rf[jj][0], RR("zeros_f")], writes=[rf[jj][2]])
            for jj in H:
                j = J[jj]
                S.op("dve", lambda jj=jj, j=j: dve_e.tensor_copy(out=aref[:, j, :, 0], in_=a3[jj][:, :, MID]), reads=[rf[jj][2]], writes=[r_aref[j]])
                S.op("dve", lambda jj=jj, j=j: dve_e.tensor_copy(out=aref[:, j, :, 2], in_=a3[jj][:, :, LB - 1]), reads=[rf[jj][2]], writes=[r_aref[j]])
                S.op("act", lambda jj=jj, j=j: act_e.activation(out=aref[:, j, :, 3], in_=a3[jj][:, :, LB - 1], func=AF.Exp), reads=[rf[jj][2]], writes=[r_aref[j]])
            for jj in H:
                j = J[jj]
                S.op("dve", lambda jj=jj, j=j: dve_e.tensor_tensor(out=w3[jj], in0=a3[jj], in1=aref[:, j, :, 0:1].broadcast_to([128, NB, LB]), op=ALU.subtract),
                     reads=[rf[jj][2], r_aref[j], rf[jj][0]], writes=[rf[jj][0]])
            for ai, (dst, src, sres, inn, ires, scale) in enumerate(((qt_, q3, "q", w3, 0, 1.0), (kt_, k3, "k", w3, 0, -1.0), (qh_, q3, "q", a3, 2, 1.0))):
                for jj in H:
                    S.op("act", lambda jj=jj: act_e.activation(out=ebuf[jj][:], in_=inn[jj], func=AF.Exp, scale=scale), reads=[rf[jj][ires]], writes=[re_[jj]])
                for jj in H:
                    j = J[jj]
                    S.op("dve", lambda jj=jj, j=j: dve_e.tensor_tensor(out=dst[:, j, :].rearrange("p (a b) -> p a b", a=NB), in0=src[jj], in1=ebuf[jj][:], op=ALU.mult),
                         reads=[r_qh[j] if sres == "q" else rf[jj][1], re_[jj]], writes=[rp[j][ai]])
            for jj in H:
                j = J[jj]
                S.op("dve", lambda jj=jj, j=j: dve_e.tensor_tensor(out=w3[jj], in0=a3[jj], in1=aref[:, j, :, 2:3].broadcast_to([128, NB, LB]), op=ALU.subtract),
                     reads=[rf[jj][2], r_aref[j], rf[jj][0]], writes=[rf[jj][0]])
            for jj in H:
                S.op("act", lambda jj=jj: act_e.activation(out=ebuf[jj][:], in_=w3[jj], func=AF.Exp, scale=-1.0), reads=[rf[jj][0]], writes=[re_[jj]])
            for jj in H:
                j = J[jj]
                S.op("dve", lambda jj=jj, j=j: dve_e.tensor_tensor(out=kh_[:, j, :].rearrange("p (a b) -> p a b", a=NB), in0=k3[jj], in1=ebuf[jj][:], op=ALU.mult),
                     reads=[rf[jj][1], re_[jj]], writes=[rp[j][3]])

        fh_unit(0)
        prep(0)
        fh_unit(1)
        for u in range(2):
            wt, r_w = ws_next()

            def ev(b, ps, prs, u=u):
                copy_op(be(prs), i_tm[0:LB, b, u * 512:(u + 1) * 512], ps, prs, [r_i[b]])
            proj_TM(T, wt, r_w, hT, r_hT, 16, nbs(), ev)
        prep(1)
        S.barrier()
        AR.reset(off_prep)
        o_all = AR.alloc((8, TT), F32)
        Pt = AR.alloc((4 * NB, LB), BF16)
        khtm = AR.alloc((4 * NB, 128), BF16)
        r_Pt = [RR("hPt", tid, b) for b in range(NB)]
        r_kh = [RR("khtm", tid, b) for b in range(NB)]
        for u in range(2):
            for b in range(NB):
                S.op("dve", lambda b=b: dve_e.memset(PSF(b, 0, 512), 0.0), writes=PR(b))
            for b in range(NB):
                cs = slice(b * LB, (b + 1) * LB)
                for jj in range(4):
                    j = 4 * u + jj
                    if LB == 128:
                        Hh = 64
                        c0 = b * LB
                        mm_group(PSF(b, jj * 128, Hh, Hh), [(kt_[:, j, c0:c0 + Hh], qt_[:, j, c0:c0 + Hh])], rp[j], PR(b))
                        mm_group(PSF(b, jj * 128 + Hh, Hh, LB), [(kt_[:, j, cs], qt_[:, j, c0 + Hh:c0 + LB])], rp[j], PR(b))
                    else:
                        mm_group(PSF(b, jj * 128, LB, LB), [(kt_[:, j, cs], qt_[:, j, cs])], rp[j], PR(b))
            for b in range(NB):
                S.op("dve", lambda b=b: dve_e.tensor_tensor(out=Pt[0:LB, b * 4:b * 4 + 4, :], in0=PSF(b, 0, 512, LB).rearrange("p (a c) -> p a c", a=4)[:, :, 0:LB],
                                                            in1=mask4[0:LB, :, 0:LB], op=ALU.mult),
                     reads=PR(b) + [RR("mask4")], writes=[r_Pt[b]])
            for b in range(NB):
                cs = slice(b * LB, (b + 1) * LB)
                bank = 4 + b // 2
                for jj in range(4):
                    j = 4 * u + jj
                    transp(PSB(bank, ((b % 2) * 4 + jj) * 128, 128, LB), kh_[:, j, cs], ident_b[:, :], rp[j] + [RR("ident_b")], PR(bank))
            for bank in range(4, 4 + (NB + 1) // 2):
                nb_here = min(2, NB - (bank - 4) * 2)
                b0 = (bank - 4) * 2
                copy_op("act", khtm[0:LB, b0 * 4:(b0 + nb_here) * 4, :], PSB(bank, 0, nb_here * 512, LB).rearrange("p (a c) -> p a c", c=128),
                        PR(bank), [r_kh[b0 + i] for i in range(nb_here)])
            for b in range(NB):
                for jj in range(4):
                    j = 4 * u + jj
                    mm_group(PSF(b, jj * 128, 128), [(khtm[0:LB, b * 4 + jj, :], i_tm[0:LB, b, j * 128:(j + 1) * 128])], [r_kh[b], r_i[b]], PR(b))
            for b in range(NB):
                cs = slice(b * LB, (b + 1) * LB)
                ob = 6 + b % 2
                if T["kind"] == "s":
                    S.dma("sp", "S_in", Sm[:, 4 * u:4 * u + 4, :], cS[b, 4 * u:4 * u + 4].rearrange("j c v -> c j v"), writes=r_S[4 * u:4 * u + 4])
                    S.op("act", lambda: act_e.copy(out=Sbf[:, 4 * u:4 * u + 4, :], in_=Sm[:, 4 * u:4 * u + 4, :]), reads=r_S[4 * u:4 * u + 4], writes=r_S[4 * u:4 * u + 4])
                for jj in range(4):
                    j = 4 * u + jj
                    mm_group(PSF(ob, jj * 128, LB), [(i_tm[0:LB, b, j * 128:(j + 1) * 128], Pt[0:LB, b * 4 + jj, :]), (Sbf[:, j, :], qh_[:, j, cs])],
                             [r_i[b], r_Pt[b], r_S[j]] + rp[j], PR(ob))
                copy_op(be(PR(ob)), o_all[:, 4 * u:4 * u + 4, cs], PSF(ob, 0, 512).rearrange("p (a c) -> p a c", a=4)[:, :, 0:LB], PR(ob), r_o[4 * u:4 * u + 4])
                for jj in range(4):
                    j = 4 * u + jj
                    S.op("dve", lambda j=j, jj=jj: dve_e.scalar_tensor_tensor(out=Sm[:, j, :], in0=Sm[:, j, :], scalar=aref[:, j, b, 3:4], in1=PSF(b, jj * 128, 128),
                                                                              op0=ALU.mult, op1=ALU.add),
                         reads=[r_S[j], r_aref[j]] + PR(b), writes=[r_S[j]])
                S.op("act", lambda: act_e.copy(out=Sbf[:, 4 * u:4 * u + 4, :], in_=Sm[:, 4 * u:4 * u + 4, :]), reads=r_S[4 * u:4 * u + 4], writes=r_S[4 * u:4 * u + 4])
                if T["kind"] == "s":
                    S.dma("sp", "S_out", o_S_s[b, 4 * u:4 * u + 4].rearrange("j c v -> c j v"), Sm[:, 4 * u:4 * u + 4, :], reads=r_S[4 * u:4 * u + 4])
        if T["kind"] == "p" and T["last"]:
            S.dma("sp", "S_out", o_S_p.rearrange("j c v -> c j v"), Sm[:], reads=r_S)
        S.barrier()
        AR.reset(off_prods)
        gsT = AR.alloc((8, TT), BF16)
        sgt = [AR.alloc((TT,), F32) for _ in range(2)]
        k = [0]
        for u in range(2):
            wt, r_w = ws_next()

            def ev(j, ps, prs, u=u):
                k[0] ^= 1
                t, rt = sgt[k[0]], RR("sgt", tid, k[0])
                S.op("act", lambda: act_e.activation(out=t[:], in_=ps, func=AF.Sigmoid), reads=prs, writes=[rt])
                S.op("dve", lambda: dve_e.tensor_tensor(out=gsT[:, u * 4 + j, :], in0=ps, in1=t[:], op=ALU.mult), reads=prs + [rt], writes=[r_gs[u * 4 + j]])
            proj_FM(T, wt, r_w, hT, r_hT, list(range(16)), nbs(), ev)
        osq = [AR.alloc((TT,), F32) for _ in range(2)]
        rstd_b = AR.alloc((TT,), F32)
        assert AR.off <= off_prep, ("hgrn tail scratch overlaps o_all", AR.off, off_prep)
        r_rstd = RR("rstd_b", tid)
        for j in range(8):
            ro = RR("osq", tid, j % 2)
            S.op("act", lambda j=j: act_e.activation(out=osq[j % 2][:], in_=o_all[:, j, :], func=AF.Square), reads=[r_o[j]], writes=[ro])
            S.op("pe", lambda j=j: pe_e.matmul(PSF(7, 0, TT), lhsT=ones_f[:, :], rhs=osq[j % 2][:], start=(j == 0), stop=(j == 7)),
                 reads=[ro, RR("ones_f")], writes=PR(7))
        S.op("act", lambda: act_e.activation(out=rstd_b[:], in_=PSF(7, 0, TT), func=AF.Ln, scale=1.0 / 1024.0, bias=eps_t[:, :]),
             reads=PR(7) + [RR("eps")], writes=[r_rstd])
        S.op("act", lambda: act_e.activation(out=rstd_b[:], in_=rstd_b[:], func=AF.Exp, scale=-0.5), reads=[r_rstd], writes=[r_rstd])
        r_ohg = R_ohg(T)
        for j in range(8):
            ro = RR("osq", tid, j % 2)
            S.op("dve", lambda j=j: dve_e.tensor_tensor(out=osq[j % 2][:], in0=o_all[:, j, :], in1=rstd_b[:], op=ALU.mult), reads=[r_o[j], r_rstd], writes=[ro])
            S.op("dve", lambda j=j: dve_e.scalar_tensor_tensor(out=ohgT[:, j, 0:TT], in0=osq[j % 2][:], scalar=par[:, PO_GHN + j:PO_GHN + j + 1], in1=gsT[:, j, :],
                                                               op0=ALU.mult, op1=ALU.mult), reads=[ro, r_par, r_gs[j]], writes=[r_ohg[j]])
        S.barrier()

    def phase_uw(ctxs):
        for c in ctxs:
            T = c["T"]
            TT, tid = T["TT"], T["id"]
            c["uT"] = AR.alloc((16, TT), BF16)
            c["sab"] = [AR.alloc((4, TT), F32) for _ in range(2)]
            c["tmp"] = [AR.alloc((TT,), F32) for _ in range(2)]
            c["r_u"] = [RR("uT", tid, i) for i in range(16)]
        bs = [0]
        for q in range(4):
            for gi in range(2):
                wt, r_w = ws_next()
                for c in ctxs:
                    T = c["T"]
                    bs[0] ^= 1

                    def ev(j, ps, prs, gi=gi, c=c, T=T):
                        S.op("act", lambda: act_e.activation(out=c["sab"][gi][:, j, :], in_=ps, func=AF.Sigmoid), reads=prs, writes=[RR("sab", T["id"], gi, j)])
                    proj_FM(T, wt, r_w, GBUF[T["kind"]]["hT"], R_hT(T), list(range(16)), bs[0], ev)
            wbab, r_wbab = ws_next()
            for c in ctxs:
                T = c["T"]
                TT, tid = T["TT"], T["id"]
                hmgT, ohgT = GBUF[T["kind"]]["hmgT"], GBUF[T["kind"]]["ohgT"]
                uT, sab, tmp, r_u = c["uT"], c["sab"], c["tmp"], c["r_u"]
                for j in range(4):
                    jc = q * 4 + j
                    bs[0] ^= 1
                    ba, bb_ = bs[0] * 4 + (j % 2) * 2, bs[0] * 4 + (j % 2) * 2 + 1
                    mm_group(PSF(ba, 0, TT), [(wbab[:, kc, j * 128:(j + 1) * 128], hmgT[:, kc, 0:TT]) for kc in range(8)], [r_wbab] + R_hmg(T), PR(ba))
                    mm_group(PSF(bb_, 0, TT), [(wbab[:, 8 + kc, j * 128:(j + 1) * 128], ohgT[:, kc, 0:TT]) for kc in range(8)], [r_wbab] + R_ohg(T), PR(bb_))
                    rt = [RR("uw_tmp", tid, i) for i in range(2)]
                    S.op("dve", lambda: dve_e.tensor_tensor(out=tmp[0][:], in0=PSF(ba, 0, TT), in1=sab[0][:, j, :], op=ALU.mult),
                         reads=PR(ba) + [RR("sab", tid, 0, j)], writes=[rt[0]])
                    S.op("dve", lambda: dve_e.tensor_tensor(out=tmp[1][:], in0=PSF(bb_, 0, TT), in1=sab[1][:, j, :], op=ALU.mult),
                         reads=PR(bb_) + [RR("sab", tid, 1, j)], writes=[rt[1]])
                    S.op("dve", lambda: dve_e.tensor_tensor(out=uT[:, jc, :], in0=tmp[0][:], in1=tmp[1][:], op=ALU.add), reads=rt, writes=[r_u[jc]])
        for nb in range(4):
            wt, r_w = ws_next()
            for c in ctxs:
                T = c["T"]
                LB = T["LB"]
                x_tm, r_x = c["x_tm"], c["r_x"]

                def ev(b, ps, prs, nb=nb, x_tm=x_tm, r_x=r_x, LB=LB):
                    S.op("dve", lambda: dve_e.tensor_tensor(out=x_tm[0:LB, b, nb * 512:(nb + 1) * 512], in0=ps, in1=x_tm[0:LB, b, nb * 512:(nb + 1) * 512], op=ALU.add),
                         reads=prs + [r_x[b]], writes=[r_x[b]])
                bs[0] ^= 1
                proj_TM(T, wt, r_w, c["uT"], c["r_u"], 16, bs[0], ev)
        S.barrier()

    def phase_ff(ctxs, next_T=None):
        off0 = AR.off
        for c in ctxs:
            T = c["T"]
            c["aT"] = AR.alloc((16, T["TT"]), BF16)
            c["sq"] = [AR.alloc((T["TT"],), F32) for _ in range(2)]
            c["r_a"] = [RR("aT", T["id"], i) for i in range(16)]
        k = [0]
        bs = [0]
        for part in range(NPART):
            for i in range(4):
                wt, r_w = ws_next()
                for c in ctxs:
                    T = c["T"]

                    def ev(j, ps, prs, i=i, c=c, T=T):
                        k[0] ^= 1
                        t, rt = c["sq"][k[0]], RR("ffsq", T["id"], k[0])
                        S.op("act", lambda: act_e.activation(out=t[:], in_=ps, func=AF.Square), reads=prs, writes=[rt])
                        S.op("dve", lambda: dve_e.scalar_tensor_tensor(out=c["aT"][:, i * 4 + j, :], in0=ps, scalar=0.0, in1=t[:], op0=ALU.is_gt, op1=ALU.mult),
                             reads=prs + [rt], writes=[c["r_a"][i * 4 + j]])
                    bs[0] ^= 1
                    proj_FM(T, wt, r_w, GBUF[T["kind"]]["hT"], R_hT(T), list(range(16)), bs[0], ev)
            for nb in range(4):
                wt, r_w = ws_next()
                for c in ctxs:
                    T = c["T"]
                    LB, NB = T["LB"], T["NB"]
                    x_tm, r_x, aT, r_a = c["x_tm"], c["r_x"], c["aT"], c["r_a"]
                    bs[0] ^= 1
                    for b in range(NB):
                        bank = bs[0] * 4 + b
                        mm_group(PSF(bank, 0, 512, LB), [(aT[:, kc, b * LB:(b + 1) * LB], wt[:, kc, :]) for kc in range(16)], [r_w] + r_a, PR(bank))
                        S.op("dve", lambda: dve_e.tensor_tensor(out=x_tm[0:LB, b, nb * 512:(nb + 1) * 512], in0=PSF(bank, 0, 512, LB),
                                                                in1=x_tm[0:LB, b, nb * 512:(nb + 1) * 512], op=ALU.add),
                             reads=PR(bank) + [r_x[b]], writes=[r_x[b]])
        S.barrier()
        AR.reset(off0)
        if next_T is not None:
            nLB, nNB = next_T["LB"], next_T["NB"]
            nx = AR.alloc((nNB, D), F32)
            nr = [RR("x_tm", next_T["id"], "n1", b) for b in range(nNB)]
            nsrc = xp[next_T["id"] * 512:(next_T["id"] + 1) * 512, :]
            for b in range(nNB):
                S.dma("sp", f"xn{b}", nx[0:nLB, b, :], nsrc[b * nLB:(b + 1) * nLB, :], writes=[nr[b]])
            next_T["xpre"] = (nx, nr)
        junk = AR.alloc((2048,), BF16)
        ssq = AR.alloc((8,), F32)
        gfin_b = AR.alloc((2048,), F32)
        r_gfin = RR("gfin_l", ctxs[0]["T"]["id"])
        S.dma("sp", "gfin", gfin_b[:], gfin_d.partition_broadcast(128), writes=[r_gfin])
        if next_T is not None:
            next_T["xpre_off"] = AR.off
        for ci, c in enumerate(ctxs):
            T = c["T"]
            LB, NB, tid = T["LB"], T["NB"], T["id"]
            x_tm, r_x, ydst = c["x_tm"], c["r_x"], c["ydst"]
            for b in range(NB):
                sc = ci * 4 + b
                xb = x_tm[0:LB, b, :]
                rs, rj = RR("f_ssq", sc), RR("f_junk")
                S.op("act", lambda: act_e.activation(out=junk[0:LB, :], in_=xb, func=AF.Square, accum_out=ssq[0:LB, sc:sc + 1]), reads=[r_x[b]], writes=[rj, rs])
                S.op("act", lambda: act_e.activation(out=ssq[0:LB, sc:sc + 1], in_=ssq[0:LB, sc:sc + 1], func=AF.Ln, scale=1.0 / D, bias=eps_t[0:LB, :]),
                     reads=[rs, RR("eps")], writes=[rs])
                S.op("act", lambda: act_e.activation(out=ssq[0:LB, sc:sc + 1], in_=ssq[0:LB, sc:sc + 1], func=AF.Exp, scale=-0.5), reads=[rs], writes=[rs])
                S.op("dve", lambda: dve_e.scalar_tensor_tensor(out=xb, in0=xb, scalar=ssq[0:LB, sc:sc + 1], in1=gfin_b[0:LB, :], op0=ALU.mult, op1=ALU.mult),
                     reads=[r_x[b], rs, r_gfin], writes=[r_x[b]])
                S.dma("sp", f"y{T['kind']}{b}", ydst[b * LB:(b + 1) * LB, :], xb, reads=[r_x[b]])
        if next_T is None:
            S.barrier()

    tiles = [dict(kind="p", id=t, LB=128, NB=4, TT=512, first=(t == 0), last=(t == 3)) for t in range(4)]
    tiles.append(dict(kind="s", id=4, LB=32, NB=2, TT=64, first=True, last=True))
    if tile_sel is not None:
        tiles = [t for t in tiles if t["id"] in tile_sel]

    def front(T):
        LB, NB, TT = T["LB"], T["NB"], T["TT"]
        if T["kind"] == "p":
            xsrc = xp[T["id"] * 512:(T["id"] + 1) * 512, :]
        else:
            xsrc = xs
            for b in range(NB):
                for jx in range(3):
                    S.dma("sp", "hist_in", hist[:, :, b, jx], cconv[b, jx].rearrange("(kc p) -> p kc", p=128), writes=[r_hist])
        if "xpre" in T:
            x_tm, r_x = T["xpre"]
            AR.reset(T["xpre_off"])
        else:
            AR.reset(0)
            x_tm = AR.alloc((NB, D), F32)
            r_x = [RR("x_tm", T["id"], "n1", b) for b in range(NB)]
            for b in range(NB):
                S.dma("sp", f"x{b}", x_tm[0:LB, b, :], xsrc[b * LB:(b + 1) * LB, :], writes=[r_x[b]])
        rmsnorm_to_FM(T, x_tm, r_x, PO_GMIX, GBUF[T["kind"]]["hT"], R_hT(T))
        S.barrier()
        phase_mlstm(T)
        phase_hgrn(T)
        if T["kind"] == "s":
            for b in range(NB):
                for jx in range(3):
                    S.dma("sp", "tail_out", o_conv_s[b, jx].rearrange("(kc p) -> p kc", p=128), tail[:, :, b, jx], reads=[r_tail])
            S.dma("sp", "m_out", o_m_s.rearrange("b h -> h b"), m_fin[0:4, 0:2], reads=[r_mfin])
        elif T["last"]:
            for jx in range(3):
                S.dma("sp", "tail_out", o_conv_p[jx].rearrange("(kc p) -> p kc", p=128), tail[:, :, 3, jx], reads=[r_tail])
            S.dma("sp", "m_out", o_m_p.rearrange("(h a) -> h a", a=1), m_fin[0:4, 0:1], reads=[r_mfin])

    def back(Ts, next_T=None):
        AR.reset(0)
        ctxs = []
        for T0 in Ts:
            if T0["kind"] == "p":
                T = T0
                xsrc = xp[T["id"] * 512:(T["id"] + 1) * 512, :]
                ydst = yp[T["id"] * 512:(T["id"] + 1) * 512, :]
            else:
                T = dict(kind="s", id=5, LB=64, NB=1, TT=64, first=True, last=True)
                xsrc, ydst = xs, ys
            LB, NB = T["LB"], T["NB"]
            x_tm = AR.alloc((NB, D), F32)
            r_x = [RR("x_tm", T["id"], "uw", b) for b in range(NB)]
            for b in range(NB):
                S.dma("sp", f"x{T['kind']}{b}", x_tm[0:LB, b, :], xsrc[b * LB:(b + 1) * LB, :], writes=[r_x[b]])
            ctxs.append(dict(T=T, x_tm=x_tm, r_x=r_x, ydst=ydst))
        off_after_x = AR.off
        phase_uw(ctxs)
        AR.reset(off_after_x)
        for c in ctxs:
            rmsnorm_to_FM(c["T"], c["x_tm"], c["r_x"], PO_GFFN, GBUF[c["T"]["kind"]]["hT"], R_hT(c["T"]), keep=True)
        S.barrier()
        AR.reset(off_after_x)
        phase_ff(ctxs, next_T)

    ids = [t["id"] for t in tiles]
    groups = []
    for T in tiles:
        if T["id"] == 4 and 3 in ids:
            groups[-1].append(T)
        else:
            groups.append([T])
    for gi, g in enumerate(groups):
        for T in g:
            front(T)
        nxt = groups[gi + 1][0] if gi + 1 < len(groups) else None
        back(g, nxt if (nxt is not None and nxt["kind"] == "p" and len(g) == 1) else None)
    S.barrier(final=True)
    assert ws["next"] == len(unit_specs), (ws["next"], len(unit_specs))
    es.close()
    return nc


def _consts():
    c = np.zeros((128, CW), np.float32)
    c[:, CO_ID:CO_ID + 128] = np.eye(128, dtype=np.float32)
    s = np.arange(128)[:, None]
    l = np.arange(128)[None, :]
    c[:, CO_MNEG:CO_MNEG + 128] = np.where(s <= l, 0.0, -30000.0)
    c[:, CO_M01:CO_M01 + 128] = (s <= l).astype(np.float32)
    for h in range(4):
        c[h, CO_SEL + h * 128:CO_SEL + (h + 1) * 128] = 1.0
        c[h, CO_BM + h] = 1.0
    return c


_NC_CACHE = {}


def kernel(x_prompt, x_sample, cache_mlstm_conv, state_mlstm_C, state_mlstm_n, state_mlstm_m, state_hgrn_S,
           g_mix, w_in, b_if, w_conv, b_conv, g_mnorm, g_hnorm, hgrn_lb_logits, w_branch_a, w_branch_b,
           w_out, g_ffn, w_ff1, w_ff2, g_final):
    f = lambda a: np.ascontiguousarray(np.asarray(a, dtype=np.float32))
    par = np.zeros((128, PW), np.float32)
    par[:, PO_GMIX:PO_GMIX + 16] = f(g_mix)[0].reshape(16, 128).T
    par[:, PO_GFFN:PO_GFFN + 16] = f(g_ffn)[0].reshape(16, 128).T
    par[:, PO_GHN:PO_GHN + 8] = f(g_hnorm)[0].reshape(8, 128).T
    par[:, PO_WCONV:PO_WCONV + 64] = f(w_conv)[0].reshape(4, 16, 128).transpose(2, 1, 0).reshape(128, 64)
    par[:, PO_BCONV:PO_BCONV + 16] = f(b_conv)[0].reshape(16, 128).T
    lbl = f(hgrn_lb_logits)
    par[:, PO_LBL:PO_LBL + 8] = lbl[0].reshape(8, 128).T
    par[:, PO_LBL + 8:PO_LBL + 16] = lbl[1].reshape(8, 128).T
    par[0:4, PO_BIF] = f(b_if)[0, 0:4]
    par[0:4, PO_BIF + 1] = f(b_if)[0, 4:8]
    consts = _consts()
    shared = {
        "w_in": f(w_in)[0], "w_ba": f(w_branch_a)[0], "w_bb": f(w_branch_b)[0], "w_out": f(w_out)[0],
        "w_ff1": f(w_ff1)[0], "w_ff2": f(w_ff2)[0], "gmn": f(g_mnorm)[0].reshape(1024), "gfin": f(g_final),
        "consts": consts, "params": par,
    }
    xpn, xsn = f(x_prompt), f(x_sample)
    cc, cCn, cnn, cmn, cSn = f(cache_mlstm_conv)[0], f(state_mlstm_C)[0], f(state_mlstm_n)[0], f(state_mlstm_m)[0], f(state_hgrn_S)[0]
    in_maps = []
    for c in range(NCORE):
        m = dict(shared)
        m["xp"] = xpn[c]
        m["xs"] = np.ascontiguousarray(xsn[2 * c:2 * c + 2].reshape(64, D))
        m["cconv"] = np.ascontiguousarray(cc[2 * c:2 * c + 2])
        m["cC"] = np.ascontiguousarray(cCn[2 * c:2 * c + 2])
        m["cn"] = np.ascontiguousarray(cnn[2 * c:2 * c + 2])
        m["cm"] = np.ascontiguousarray(cmn[2 * c:2 * c + 2])
        m["cS"] = np.ascontiguousarray(cSn[2 * c:2 * c + 2])
        in_maps.append(m)
    if "nc" not in _NC_CACHE:
        _NC_CACHE["nc"] = build_program()
    nc = _NC_CACHE["nc"]
    res = run_bass_kernel_spmd(nc, in_maps, core_ids=list(range(NCORE)))
    rs = res.results
    g = lambda k: np.stack([np.asarray(r[k], dtype=np.float32) for r in rs])
    y_prompt = g("yp").reshape(8, SEQ, D)
    y_sample = g("ys").reshape(8, 2, 32, D).reshape(16, 32, D)
    out = (
        y_prompt, y_sample,
        g("o_conv_p").reshape(1, 8, 3, D), g("o_C_p").reshape(1, 8, 4, 256, 256), g("o_n_p").reshape(1, 8, 4, 256),
        g("o_m_p").reshape(1, 8, 4), g("o_S_p").reshape(1, 8, 8, 128, 128),
        g("o_conv_s").reshape(1, 16, 3, D), g("o_C_s").reshape(1, 16, 4, 256, 256), g("o_n_s").reshape(1, 16, 4, 256),
        g("o_m_s").reshape(1, 16, 4), g("o_S_s").reshape(1, 16, 8, 128, 128),
    )
    return out
```

```python
import numpy as np
from contextlib import ExitStack
import concourse.bass as bass
import concourse.mybir as mybir
from concourse.bass_utils import run_bass_kernel_spmd

F32 = mybir.dt.float32
BF16 = mybir.dt.bfloat16
AF = mybir.ActivationFunctionType
ALU = mybir.AluOpType

D = 2048
NCORE = 8
SEQ = 2048
N_IN = 12296
EPS = 1e-6
C_QK, C_V, C_OM, C_IF, C_FH, C_IH, C_QH, C_GH, C_GA, C_GB = 0, 2048, 3072, 4096, 4104, 5128, 6152, 7176, 8200, 10248
NPART = 4
ML_ORDER = [("K", 0), ("V", 0), ("K", 1), ("OM", 0), ("Q", 0), ("V", 1), ("Q", 1), ("OM", 1)]
NSLOT = 3
ARENA_F32 = 22784

CO_ID, CO_MNEG, CO_M01, CO_SEL, CO_BM = 0, 128, 256, 384, 896
CW = 912
PO_GMIX, PO_GFFN, PO_GHN, PO_WCONV, PO_BCONV, PO_LBL, PO_BIF = 0, 16, 32, 40, 104, 120, 136
PW = 138


class Res:
    __slots__ = ("name", "lw", "rd", "excl", "bank")

    def __init__(self, name, excl=False, bank=-1):
        self.name = name
        self.lw = None
        self.rd = {}
        self.excl = excl
        self.bank = bank


class Sched:
    def __init__(self, nc, es):
        self.nc = nc
        self.es = es
        self.eng = {"pe": nc.tensor, "act": nc.scalar, "dve": nc.vector, "pool": nc.gpsimd, "sp": nc.sync}
        self.sem = {}
        self.cnt = {}
        for e in ("pe", "act", "dve"):
            self.sem[e] = es.enter_context(nc.semaphore("sem_" + e))
            self.cnt[e] = 0
        self.seen = {e: {} for e in self.eng}
        self.dsem = {}
        self.dcnt = {}
        self.nowait_dma = set()
        self.marks = []
        self.label = ""

    def _wait(self, e, tok):
        key, sem, val = tok
        if self.seen[e].get(key, 0) >= val:
            return
        self.eng[e].wait_ge(sem, val)
        self.seen[e][key] = val

    def _deps(self, e, reads, writes):
        deps = {}

        def add(tok, kind):
            if tok is None:
                return
            key = tok[0]
            if key == e and e == "pe":
                return
            if key not in deps or deps[key][2] < tok[2]:
                deps[key] = tok

        for r in reads:
            add(r.lw, "raw")
        for r in writes:
            add(r.lw, "waw")
            for t in r.rd.values():
                add(t, "war")
        for tok in deps.values():
            self._wait(e, tok)

    def _commit(self, tok, reads, writes):
        for r in reads:
            r.rd[tok[0]] = tok
        for r in writes:
            r.lw = tok
            r.rd = {}

    def op(self, e, fn, reads=(), writes=()):
        if any(r.excl for r in reads):
            writes = list(writes) + [r for r in reads if r.excl]
            reads = [r for r in reads if not r.excl]
        self._deps(e, reads, writes)
        ins = fn()
        self.cnt[e] += 1
        ins.then_inc(self.sem[e], 1)
        tok = (e, self.sem[e], self.cnt[e])
        self._commit(tok, reads, writes)
        return tok

    def dma(self, q, semname, out, in_, reads=(), writes=(), barrier=True):
        if semname not in self.dsem:
            self.dsem[semname] = self.es.enter_context(self.nc.semaphore("dq_" + semname))
            self.dcnt[semname] = 0
            if not barrier:
                self.nowait_dma.add(semname)
        self._deps(q, reads, writes)
        ins = self.eng[q].dma_start(out=out, in_=in_)
        self.dcnt[semname] += 16
        ins.then_inc(self.dsem[semname], 16)
        tok = ("dma_" + semname, self.dsem[semname], self.dcnt[semname])
        self._commit(tok, reads, writes)
        return tok

    def barrier(self, final=False):
        self.marks.append((self.label, dict(self.cnt)))
        toks = [(e, self.sem[e], self.cnt[e]) for e in ("pe", "act", "dve") if self.cnt[e] > 0]
        for n, s in self.dsem.items():
            if (final or n not in self.nowait_dma) and self.dcnt[n] > 0:
                toks.append(("dma_" + n, s, self.dcnt[n]))
        for e in ("pe", "act", "dve", "sp"):
            for t in toks:
                if t[0] != e:
                    self._wait(e, t)


class Arena:
    def __init__(self, ap_f32, nf32):
        self.ap = ap_f32
        self.n = nf32
        self.off = 0

    def reset(self, off=0):
        self.off = off

    def alloc(self, shape, dt, parts=128):
        nel = 1
        for s in shape:
            nel *= s
        nf = nel if dt == F32 else (nel + 1) // 2
        nf = (nf + 1) // 2 * 2
        assert self.off + nf <= self.n, ("arena overflow", self.off, nf, self.n)
        self.hw = max(getattr(self, "hw", 0), self.off + nf)
        v = self.ap[0:parts, self.off:self.off + nf]
        self.off += nf
        if dt != F32:
            v = v.bitcast(dt)
        v = v[:, 0:nel]
        if len(shape) == 2:
            v = v.rearrange("p (a b) -> p a b", a=shape[0])
        elif len(shape) == 3:
            v = v.rearrange("p (a b c) -> p a b c", a=shape[0], b=shape[1])
        return v


def build_program(tile_sel=None):
    nc = bass.Bass("TRN2", target_bir_lowering=False)
    es = ExitStack()
    dt_in = lambda name, shape: nc.dram_tensor(name, list(shape), F32, kind="ExternalInput").ap()
    dt_out = lambda name, shape: nc.dram_tensor(name, list(shape), F32, kind="ExternalOutput").ap()
    xp = dt_in("xp", (SEQ, D))
    xs = dt_in("xs", (64, D))
    cconv = dt_in("cconv", (2, 3, D))
    cC = dt_in("cC", (2, 4, 256, 256))
    cn = dt_in("cn", (2, 4, 256))
    cm = dt_in("cm", (2, 4))
    cS = dt_in("cS", (2, 8, 128, 128))
    w_in = dt_in("w_in", (D, N_IN))
    w_ba = dt_in("w_ba", (1024, D))
    w_bb = dt_in("w_bb", (1024, D))
    w_out = dt_in("w_out", (D, D))
    w_ff1 = dt_in("w_ff1", (D, 4 * D))
    w_ff2 = dt_in("w_ff2", (4 * D, D))
    gmn_d = dt_in("gmn", (1024,))
    gfin_d = dt_in("gfin", (D,))
    consts_d = dt_in("consts", (128, CW))
    params_d = dt_in("params", (128, PW))
    yp = dt_out("yp", (SEQ, D))
    ys = dt_out("ys", (64, D))
    o_conv_p = dt_out("o_conv_p", (3, D))
    o_C_p = dt_out("o_C_p", (4, 256, 256))
    o_n_p = dt_out("o_n_p", (4, 256))
    o_m_p = dt_out("o_m_p", (4,))
    o_S_p = dt_out("o_S_p", (8, 128, 128))
    o_conv_s = dt_out("o_conv_s", (2, 3, D))
    o_C_s = dt_out("o_C_s", (2, 4, 256, 256))
    o_n_s = dt_out("o_n_s", (2, 4, 256))
    o_m_s = dt_out("o_m_s", (2, 4))
    o_S_s = dt_out("o_S_s", (2, 8, 128, 128))

    es.enter_context(nc.allow_non_contiguous_dma(reason="small state / tail transposing DMAs"))
    sb = lambda name, shape, dt: es.enter_context(nc.sbuf_tensor(name, list(shape), dt))
    wslot = [sb(f"wslot{i}", (128, 16, 512), BF16) for i in range(NSLOT)]
    hT = sb("hT", (128, 16, 512), BF16)
    hmgT = sb("hmgT", (128, 8, 512), BF16)
    ohgT = sb("ohgT", (128, 8, 512), BF16)
    hT_s = sb("hT_s", (128, 16, 64), BF16)
    hmgT_s = sb("hmgT_s", (128, 8, 64), BF16)
    ohgT_s = sb("ohgT_s", (128, 8, 64), BF16)
    GBUF = {"p": dict(hT=hT, hmgT=hmgT, ohgT=ohgT), "s": dict(hT=hT_s, hmgT=hmgT_s, ohgT=ohgT_s)}
    cst = sb("cst", (128, CW), F32)
    par = sb("par", (128, PW), F32)
    gmn_b = sb("gmn_b", (128, 1024), F32)
    ident_b = sb("ident_b", (128, 128), BF16)
    ones_b = sb("ones_b", (128, 128), BF16)
    maskneg_b = sb("maskneg_b", (128, 128), BF16)
    mask4 = sb("mask4", (128, 4, 128), BF16)
    ones_f = sb("ones_f", (128, 128), F32)
    zeros_f = sb("zeros_f", (128, 128), F32)
    wif = sb("wif", (128, 16, 8), BF16)
    lb_t = sb("lb_t", (128, 8), F32)
    oml_t = sb("oml_t", (128, 8), F32)
    noml_t = sb("noml_t", (128, 8), F32)
    Cm = sb("Cm", (128, 4, 2, 256), F32)
    Cbf = sb("Cbf", (128, 4, 2, 256), BF16)
    nm = sb("nm", (128, 4, 2), F32)
    nbf = sb("nbf", (128, 4, 2), BF16)
    Sm = sb("Sm", (128, 8, 128), F32)
    Sbf = sb("Sbf", (128, 8, 128), BF16)
    hist = sb("hist", (128, 16, 4, 3), F32)
    tail = sb("tail", (128, 16, 4, 3), F32)
    m_in = sb("m_in", (4, 8), F32)
    m_fin = sb("m_fin", (4, 8), F32)
    arena_t = sb("arena", (128, ARENA_F32), F32)
    AR = Arena(arena_t[:], ARENA_F32)
    psb = [es.enter_context(nc.psum_tensor(f"psb{i}", [128, 512], F32)) for i in range(8)]
    psbank = [Res(f"psbank{b}", excl=True, bank=b) for b in range(8)]

    def be(prs):
        return "act" if prs[0].bank % 2 == 0 else "dve"

    def PSF(bank, off, n, parts=128):
        return psb[bank][0:parts, off:off + n]

    def PSB(bank, off, n, parts=128):
        return psb[bank][0:parts, :].bitcast(BF16)[:, off:off + n]

    def PR(bank, q0=0, q1=4):
        return [psbank[bank]]

    S = Sched(nc, es)
    R = {}

    def R_hT(T):
        return [RR("hT", T["kind"], kc) for kc in range(16)]

    def R_hmg(T):
        return [RR("hmgT", T["kind"], c) for c in range(8)]

    def R_ohg(T):
        return [RR("ohgT", T["kind"], c) for c in range(8)]

    def RR(*key):
        if key not in R:
            R[key] = Res(str(key))
        return R[key]

    act_e, dve_e, pe_e = nc.scalar, nc.vector, nc.tensor
    ident_f = cst[:, CO_ID:CO_ID + 128]
    maskneg = cst[:, CO_MNEG:CO_MNEG + 128]
    mask01 = cst[:, CO_M01:CO_M01 + 128]
    selm = cst[:, CO_SEL:CO_SEL + 512]
    bmask = cst[:, CO_BM:CO_BM + 16]
    r_cst, r_par = RR("cst"), RR("par")

    unit_specs = []
    ukey = [0]
    wcache = nc.dram_tensor("wcache", [64, 128, 16 * 512], BF16, kind="Internal").ap()

    def add_unit(parts):
        unit_specs.append((ukey[0], parts))
        ukey[0] += 1

    def wv(wd, r0, nr, c0, ncols=512):
        return wd[r0:r0 + nr, c0:c0 + ncols].rearrange("(kc p) n -> p kc n", p=128)

    def front_units():
        ukey[0] = 0
        for (kind, u) in ML_ORDER:
            c0 = {"V": C_V, "OM": C_OM, "K": C_QK + 1024, "Q": C_QK}[kind]
            add_unit([(wv(w_in, 0, D, c0 + u * 512), 0, 16)])
        for c in (C_QH, C_FH, C_IH, C_GH):
            for u in range(2):
                add_unit([(wv(w_in, 0, D, c + u * 512), 0, 16)])

    def back_units():
        ukey[0] = 16
        for q in range(4):
            add_unit([(wv(w_in, 0, D, C_GA + q * 512), 0, 16)])
            add_unit([(wv(w_in, 0, D, C_GB + q * 512), 0, 16)])
            add_unit([(wv(w_ba, 0, 1024, q * 512), 0, 8), (wv(w_bb, 0, 1024, q * 512), 8, 8)])
        for nb in range(4):
            add_unit([(wv(w_out, 0, D, nb * 512), 0, 16)])
        for part in range(NPART):
            for i in range(4):
                add_unit([(wv(w_ff1, 0, D, part * 2048 + i * 512), 0, 16)])
            for nb in range(4):
                add_unit([(wv(w_ff2, part * 2048, 2048, nb * 512), 0, 16)])

    _ids = list(range(5)) if tile_sel is None else sorted(tile_sel)
    for _i in _ids:
        front_units()
        if not (_i == 3 and 4 in _ids):
            back_units()
    ws = {"issued": 0, "next": 0}
    slot_res = [RR("wslot", i) for i in range(NSLOT)]

    n_use, occ = {}, []
    for _i, (_k, _p) in enumerate(unit_specs):
        occ.append(n_use.get(_k, 0))
        n_use[_k] = n_use.get(_k, 0) + 1

    def cache_occ(key):
        c = 0 if key < 16 else (key - 16) % 3
        return min(c, max(n_use[key] - 2, 0))

    def ws_issue(i):
        slot = i % NSLOT
        key, parts = unit_specs[i]
        if occ[i] <= cache_occ(key):
            for (src, kc0, nkc) in parts:
                S.dma("pool", f"w{slot}", wslot[slot][:, kc0:kc0 + nkc, :], src, writes=[slot_res[slot]], barrier=False)
        else:
            S.dma("pool", f"w{slot}", wslot[slot][:].rearrange("p a b -> p (a b)"), wcache[key], reads=[RR("wcache", key)],
                  writes=[slot_res[slot]], barrier=False)

    def ws_next():
        i = ws["next"]
        ws["next"] += 1
        while ws["issued"] < min(len(unit_specs), i + NSLOT):
            ws_issue(ws["issued"])
            ws["issued"] += 1
        slot = i % NSLOT
        key = unit_specs[i][0]
        if occ[i] == cache_occ(key) and n_use[key] > occ[i] + 1:
            S.dma("sp", f"wb{slot}", wcache[key], wslot[slot][:].rearrange("p a b -> p (a b)"), reads=[slot_res[slot]],
                  writes=[RR("wcache", key)], barrier=False)
        return wslot[slot], slot_res[slot]

    S.dma("sp", "cst", cst[:], consts_d, writes=[r_cst])
    S.dma("sp", "par", par[:], params_d, writes=[r_par])
    S.dma("sp", "gmn", gmn_b[:], gmn_d.partition_broadcast(128), writes=[RR("gmn_b")])
    S.dma("pool", "wif", wif[:], w_in[:, C_IF:C_IF + 8].rearrange("(kc p) n -> p kc n", p=128), writes=[RR("wif")])
    S.op("dve", lambda: dve_e.tensor_copy(out=ident_b[:], in_=ident_f), reads=[r_cst], writes=[RR("ident_b")])
    S.op("dve", lambda: dve_e.memset(ones_b[:], 1.0), writes=[RR("ones_b")])
    S.op("dve", lambda: dve_e.tensor_copy(out=maskneg_b[:], in_=maskneg), reads=[r_cst], writes=[RR("maskneg_b")])
    for i4 in range(4):
        S.op("dve", lambda i4=i4: dve_e.tensor_copy(out=mask4[:, i4, :], in_=mask01), reads=[r_cst], writes=[RR("mask4")])
    S.op("dve", lambda: dve_e.memset(ones_f[:], 1.0), writes=[RR("ones_f")])
    S.op("dve", lambda: dve_e.memset(zeros_f[:], 0.0), writes=[RR("zeros_f")])
    r_cpool = [r_cst, r_par, RR("ident_b"), RR("ones_b"), RR("ones_f"), RR("zeros_f"), RR("gmn_b"), RR("wif")]
    S.op("dve", lambda: dve_e.tensor_tensor(out=oml_t[:], in0=par[:, PO_LBL:PO_LBL + 8], in1=par[:, PO_LBL + 8:PO_LBL + 16],
                                            op=ALU.subtract), reads=[r_par], writes=[RR("oml")])
    S.op("act", lambda: act_e.activation(out=lb_t[:], in_=oml_t[:], func=AF.Sigmoid), reads=[RR("oml")], writes=[RR("lb")])
    S.op("dve", lambda: dve_e.tensor_scalar(out=oml_t[:], in0=lb_t[:], scalar1=-1.0, scalar2=1.0, op0=ALU.mult, op1=ALU.add),
         reads=[RR("lb")], writes=[RR("oml")])
    S.op("dve", lambda: dve_e.tensor_scalar(out=noml_t[:], in0=oml_t[:], scalar1=-1.0, scalar2=None, op0=ALU.mult),
         reads=[RR("oml")], writes=[RR("noml")])
    r_cpool += [RR("lb"), RR("oml"), RR("noml")]
    r_st = [RR("state_ml", h) for h in range(4)]
    r_cbf = [[RR("cbf", p, h) for h in range(4)] for p in range(3)]
    r_nbf = [RR("nbf", p) for p in range(3)]
    r_S = [RR("state_hg", j) for j in range(8)]
    r_hist, r_tail, r_min, r_mfin = RR("hist"), RR("tail"), RR("m_in"), RR("m_fin")
    S.op("dve", lambda: dve_e.memset(Cm[:], 0.0), writes=r_st)
    S.op("dve", lambda: dve_e.memset(nm[:], 0.0), writes=r_st)
    S.op("dve", lambda: dve_e.memset(Cbf[:], 0.0), writes=r_cbf[0])
    S.op("dve", lambda: dve_e.memset(nbf[:], 0.0), writes=[r_nbf[0]])
    S.op("dve", lambda: dve_e.memset(Sm[:], 0.0), writes=r_S)
    S.op("dve", lambda: dve_e.memset(Sbf[:], 0.0), writes=r_S)
    S.op("dve", lambda: dve_e.memset(hist[:], 0.0), writes=[r_hist])
    S.op("dve", lambda: dve_e.memset(m_in[:], 0.0), writes=[r_min])
    S.barrier()

    rr_flip = {"i": 0}

    def alt():
        rr_flip["i"] ^= 1
        return "act" if rr_flip["i"] else "dve"

    def copy_op(e, out, in_, reads, writes):
        if e == "act":
            return S.op("act", lambda: act_e.copy(out=out, in_=in_), reads=reads, writes=writes)
        return S.op("dve", lambda: dve_e.tensor_copy(out=out, in_=in_), reads=reads, writes=writes)

    def scale_op(e, out, in_, sc_ap, reads, writes):
        if e == "act":
            return S.op("act", lambda: act_e.activation(out=out, in_=in_, func=AF.Copy, scale=sc_ap), reads=reads, writes=writes)
        return S.op("dve", lambda: dve_e.tensor_scalar(out=out, in0=in_, scalar1=sc_ap, scalar2=None, op0=ALU.mult),
                    reads=reads, writes=writes)

    def mm_group(out, pairs, reads, writes, start=True, stop=True):
        def fn():
            ins = None
            n = len(pairs)
            for i, (l, r) in enumerate(pairs):
                ins = pe_e.matmul(out, lhsT=l, rhs=r, start=(start and i == 0), stop=(stop and i == n - 1))
            return ins
        return S.op("pe", fn, reads=reads, writes=writes)

    def transp(out, in_, ident, reads, writes):
        return S.op("pe", lambda: pe_e.transpose(out=out, in_=in_, identity=ident), reads=reads, writes=writes)

    def rmsnorm_to_FM(T, x_tm, r_x, gcol, dstT, r_dst, keep=False):
        LB, NB = T["LB"], T["NB"]
        off0 = AR.off
        xn = [AR.alloc((2048,), BF16) for _ in range(2)]
        ssq = AR.alloc((8,), F32)
        for b in range(NB):
            xb = x_tm[0:LB, b, :]
            rs, rx = RR("n_ssq", T["id"], id(dstT), b), RR("n_xn", T["id"], id(dstT), b % 2)
            S.op("act", lambda: act_e.activation(out=xn[b % 2][0:LB, :], in_=xb, func=AF.Square, accum_out=ssq[0:LB, b:b + 1]),
                 reads=[r_x[b]], writes=[rx, rs])
            S.op("act", lambda: act_e.activation(out=ssq[0:LB, b:b + 1], in_=ssq[0:LB, b:b + 1], func=AF.Ln,
                                                 scale=1.0 / D, bias=eps_t[0:LB, :]), reads=[rs], writes=[rs])
            S.op("act", lambda: act_e.activation(out=ssq[0:LB, b:b + 1], in_=ssq[0:LB, b:b + 1], func=AF.Exp, scale=-0.5), reads=[rs], writes=[rs])
            xnb = xn[b % 2]
            scale_op("dve", xnb[0:LB, :], xb, ssq[0:LB, b:b + 1], [r_x[b], rs], [rx])
            for half in range(2):
                bank = (b % 4) * 2 + half
                for c8 in range(8):
                    kc = half * 8 + c8
                    transp(PSB(bank, c8 * 128, LB), xnb[0:LB, kc * 128:(kc + 1) * 128], ident_b[0:LB, 0:LB],
                           [rx] + r_cpool[2:3], PR(bank, c8 // 2, c8 // 2 + 1))
                gb = par[:, gcol + half * 8:gcol + half * 8 + 8].unsqueeze(2).broadcast_to([128, 8, LB])
                S.op("dve", lambda: dve_e.tensor_tensor(out=dstT[:, half * 8:half * 8 + 8, b * LB:(b + 1) * LB],
                                                        in0=PSB(bank, 0, 1024).rearrange("p (a c) -> p a c", a=8)[:, :, 0:LB], in1=gb, op=ALU.mult),
                     reads=PR(bank) + [r_par], writes=r_dst[half * 8:half * 8 + 8])
        if not keep:
            AR.reset(off0)

    eps_t = sb("eps_t", (128, 1), F32)
    S.op("dve", lambda: dve_e.memset(eps_t[:], EPS), writes=[RR("eps")])
    S.barrier()

    def proj_TM(T, wt, r_w, srcT, r_src, nkc, bankset, evac):
        LB, NB = T["LB"], T["NB"]
        for b in range(NB):
            bank = bankset * 4 + b
            mm_group(PSF(bank, 0, 512, LB), [(srcT[:, kc, b * LB:(b + 1) * LB], wt[:, kc, :]) for kc in range(nkc)],
                     [r_w] + r_src, PR(bank))
            evac(b, PSF(bank, 0, 512, LB), PR(bank))

    def proj_FM(T, wt, r_w, srcT, r_src, kcs, bankset, evac, wkc0=0):
        TT = T["TT"]
        for j in range(4):
            bank = bankset * 4 + j
            mm_group(PSF(bank, 0, TT), [(wt[:, wkc0 + i, j * 128:(j + 1) * 128], srcT[:, kc, 0:TT]) for i, kc in enumerate(kcs)],
                     [r_w] + r_src, PR(bank))
            evac(j, PSF(bank, 0, TT), PR(bank))

    def phase_mlstm(T):
        LB, NB, TT, tid = T["LB"], T["NB"], T["TT"], T["id"]
        AR.reset(0)
        v_tm = AR.alloc((NB, 1024), BF16)
        so_tm = AR.alloc((NB, 1024), BF16)
        kT = AR.alloc((8, TT), BF16)
        qT = AR.alloc((8, TT), BF16)
        r_v = [RR("v_tm", tid, b) for b in range(NB)]
        r_so = [RR("so_tm", tid, b) for b in range(NB)]
        r_kT = [RR("kT", tid, c) for c in range(8)]
        r_qT = [RR("qT", tid, c) for c in range(8)]
        r_hT = R_hT(T)
        hT = GBUF[T["kind"]]["hT"]
        hmgT = GBUF[T["kind"]]["hmgT"]
        bs = [0]

        def nbs():
            bs[0] ^= 1
            return bs[0]
        off_tmp = AR.off
        sg_tmp = [AR.alloc((512,), F32) for _ in range(2)]
        k = [0]
        qkpad = [AR.alloc((NB, LB + 3), F32) for _ in range(2)]
        accb = [AR.alloc((NB, LB), F32) for _ in range(2)]
        sgb = [AR.alloc((NB, LB), F32) for _ in range(2)]
        cc = [0]
        for (kind, u) in ML_ORDER:
            wt, r_w = ws_next()
            if kind == "V":
                def ev(b, ps, prs, u=u):
                    copy_op(be(prs), v_tm[0:LB, b, u * 512:(u + 1) * 512], ps, prs, [r_v[b]])
                proj_TM(T, wt, r_w, hT, r_hT, 16, nbs(), ev)
            elif kind == "OM":
                def ev(b, ps, prs, u=u):
                    k[0] ^= 1
                    t, rt = sg_tmp[k[0]], RR("sg_tmp", tid, k[0])
                    S.op("act", lambda: act_e.activation(out=t[0:LB, :], in_=ps, func=AF.Sigmoid), reads=prs, writes=[rt])
                    S.op("dve", lambda: dve_e.tensor_tensor(out=so_tm[0:LB, b, u * 512:(u + 1) * 512], in0=t[0:LB, :],
                                                            in1=gmn_b[0:LB, u * 512:(u + 1) * 512], op=ALU.mult),
                         reads=[rt, RR("gmn_b")], writes=[r_so[b]])
                proj_TM(T, wt, r_w, hT, r_hT, 16, nbs(), ev)
            else:
                isq = 1 if kind == "Q" else 0

                def ev(j, ps, prs, u=u, isq=isq):
                    cc[0] ^= 1
                    i = cc[0]
                    pad, acc, sg = qkpad[i], accb[i], sgb[i]
                    rp, ra, rs = RR("qkpad", tid, i), RR("accb", tid, i), RR("sgb", tid, i)
                    c8 = u * 4 + j
                    c16 = c8 if isq else 8 + c8
                    wc = lambda ii: par[:, PO_WCONV + c16 * 4 + ii:PO_WCONV + c16 * 4 + ii + 1]
                    S.op("act", lambda: act_e.copy(out=pad[:, :, 3:3 + LB], in_=ps.rearrange("p (a b) -> p a b", a=NB)),
                         reads=prs, writes=[rp])
                    S.op("dve", lambda: dve_e.tensor_copy(out=pad[:, :, 0:3], in_=hist[:, c16, 0:NB, :]), reads=[r_hist], writes=[rp])
                    if T["kind"] == "p":
                        S.op("dve", lambda: dve_e.tensor_copy(out=pad[:, 1:NB, 0:3], in_=pad[:, 0:NB - 1, LB:LB + 3]),
                             reads=[rp], writes=[rp])
                        S.op("dve", lambda: dve_e.tensor_copy(out=hist[:, c16, 0, :], in_=pad[:, NB - 1, LB:LB + 3]),
                             reads=[rp], writes=[r_hist])
                    S.op("dve", lambda: dve_e.tensor_copy(out=tail[:, c16, 0:NB, :], in_=pad[:, :, LB:LB + 3]), reads=[rp], writes=[r_tail])
                    S.op("act", lambda: act_e.activation(out=acc[:], in_=pad[:, :, 3:3 + LB], func=AF.Identity, scale=wc(3),
                                                         bias=par[:, PO_BCONV + c16:PO_BCONV + c16 + 1]),
                         reads=[rp, r_par], writes=[ra])
                    for ii in (2, 1, 0):
                        S.op("dve", lambda ii=ii: dve_e.scalar_tensor_tensor(out=acc[:], in0=pad[:, :, ii:ii + LB], scalar=wc(ii), in1=acc[:],
                                                                             op0=ALU.mult, op1=ALU.add),
                             reads=[rp, ra, r_par], writes=[ra])
                    S.op("act", lambda: act_e.activation(out=sg[:], in_=acc[:], func=AF.Sigmoid), reads=[ra], writes=[rs])
                    dst = (qT if isq else kT)[:, c8, :].rearrange("p (a b) -> p a b", a=NB)
                    rdst = (r_qT if isq else r_kT)[c8]
                    scl = 1.0 if isq else 1.0 / 16.0
                    S.op("dve", lambda: dve_e.scalar_tensor_tensor(out=dst, in0=acc[:], scalar=scl, in1=sg[:], op0=ALU.mult, op1=ALU.mult),
                         reads=[ra, rs], writes=[rdst])
                proj_FM(T, wt, r_w, hT, r_hT, list(range(16)), nbs(), ev)
        gsb = AR.alloc((2, TT), F32, parts=4)
        r_g = RR("gsb", tid)
        for gi in range(2):
            mm_group(PSF(gi, 0, TT, 4), [(wif[:, kc, gi * 4:gi * 4 + 4], hT[:, kc, 0:TT]) for kc in range(16)],
                     [RR("wif")] + r_hT, PR(gi))
            S.op("act", lambda gi=gi: act_e.activation(out=gsb[0:4, gi, :], in_=PSF(gi, 0, TT, 4), func=AF.Identity,
                                                       bias=par[0:4, PO_BIF + gi:PO_BIF + gi + 1]), reads=PR(gi) + [r_par], writes=[r_g])
        lf = AR.alloc((TT,), F32, parts=4)
        t1 = AR.alloc((TT,), F32, parts=4)
        r_lf, r_t1 = RR("lf", tid), RR("t1", tid)
        S.op("act", lambda: act_e.activation(out=t1[0:4, :], in_=gsb[0:4, 1, :], func=AF.Abs), reads=[r_g], writes=[r_t1])
        S.op("act", lambda: act_e.activation(out=t1[0:4, :], in_=t1[0:4, :], func=AF.Exp, scale=-1.0), reads=[r_t1], writes=[r_t1])
        S.op("act", lambda: act_e.activation(out=t1[0:4, :], in_=t1[0:4, :], func=AF.Ln, bias=ones_f[0:4, 0:1]), reads=[r_t1, RR("ones_f")], writes=[r_t1])
        S.op("dve", lambda: dve_e.tensor_single_scalar(out=lf[0:4, :], in_=gsb[0:4, 1, :], scalar=0.0, op=ALU.min), reads=[r_g], writes=[r_lf])
        S.op("dve", lambda: dve_e.tensor_tensor(out=lf[0:4, :], in0=lf[0:4, :], in1=t1[0:4, :], op=ALU.subtract), reads=[r_lf, r_t1], writes=[r_lf])
        rows2 = [AR.alloc((8, LB), F32, parts=4) for _ in range(2)]
        rexp = AR.alloc((16,), F32, parts=4)
        gtm2 = [AR.alloc((16,), F32) for _ in range(2)]
        decb2 = [AR.alloc((4,), F32) for _ in range(2)]
        CB = [Cbf, AR.alloc((4, 2, 256), BF16), AR.alloc((4, 2, 256), BF16)]
        NBF = [nbf, AR.alloc((4, 2), BF16), AR.alloc((4, 2), BF16)]
        Dt = AR.alloc((4, LB), F32)
        Pt = AR.alloc((4, LB), BF16)
        intra_sb = AR.alloc((4, 256), F32)
        hnum = AR.alloc((4, 256), F32)
        sqj = AR.alloc((256,), BF16)
        hm_tm = AR.alloc((1024,), BF16)
        kw = AR.alloc((4, 256), BF16)
        den_sb = AR.alloc((8,), F32)
        sm = AR.alloc((8, 4), F32)
        trs = AR.alloc((4, 2, 256), F32) if T["kind"] == "s" else None
        r_rexp = RR("rexp", tid)
        r_rows2 = [RR("rows", tid, p) for p in range(2)]
        r_gtm2 = [RR("gtm", tid, p) for p in range(2)]
        r_decb2 = [RR("decb", tid, p) for p in range(2)]
        r_den, r_sm = RR("den_sb", tid), RR("sm", tid)
        r_Dt = [RR("Dt", tid, h) for h in range(4)]
        r_Pt = [RR("Pt", tid, h) for h in range(4)]
        r_is = [RR("intra_sb", tid, h) for h in range(4)]
        r_hn = [RR("hnum", tid, h) for h in range(4)]
        r_kw = [RR("kw", tid, h) for h in range(4)]
        r_hm = [RR("hm_tm", tid, h) for h in range(4)]
        r_sqj = RR("sqj", tid)
        r_hmg = R_hmg(T)

        def upd(b):
            cs = slice(b * LB, (b + 1) * LB)
            p = b % 2
            rows, gtm, decb = rows2[p], gtm2[p], decb2[p]
            r_rows, r_gtm, r_decb = r_rows2[p], r_gtm2[p], r_decb2[p]
            if T["kind"] == "s":
                load_state_ml(T, b, trs, CB[b % 3], r_cbf[b % 3], NBF[b % 3], r_nbf[b % 3])
            S.op("dve", lambda: dve_e.tensor_tensor_scan(out=rows[0:4, 0, :], data0=lf[0:4, cs], data1=zeros_f[0:4, 0:LB], initial=0.0,
                                                         op0=ALU.add, op1=ALU.add), reads=[r_lf, RR("zeros_f")], writes=[r_rows])
            S.op("dve", lambda: dve_e.tensor_tensor(out=rows[0:4, 1, :], in0=gsb[0:4, 0, cs], in1=rows[0:4, 0, :], op=ALU.subtract),
                 reads=[r_g, r_rows], writes=[r_rows])
            S.op("dve", lambda: dve_e.tensor_tensor_scan(out=rows[0:4, 2, :], data0=rows[0:4, 1, :], data1=rows[0:4, 1, :],
                                                         initial=m_in[0:4, b:b + 1], op0=ALU.max, op1=ALU.max),
                 reads=[r_rows, r_min], writes=[r_rows])
            S.op("dve", lambda: dve_e.tensor_scalar(out=rows[0:4, 3, :], in0=rows[0:4, 2, :], scalar1=-1.0, scalar2=None, op0=ALU.mult),
                 reads=[r_rows], writes=[r_rows])
            S.op("act", lambda: act_e.activation(out=rows[0:4, 4, :], in_=rows[0:4, 2, :], func=AF.Exp, scale=-1.0, bias=m_in[0:4, b:b + 1]),
                 reads=[r_rows, r_min], writes=[r_rows])
            S.op("dve", lambda: dve_e.tensor_tensor(out=rows[0:4, 7, :], in0=rows[0:4, 0, :], in1=rows[0:4, 2, :], op=ALU.add),
                 reads=[r_rows], writes=[r_rows])
            S.op("act", lambda: act_e.activation(out=rows[0:4, 5, :], in_=rows[0:4, 7, :], func=AF.Exp, scale=-1.0), reads=[r_rows], writes=[r_rows])
            S.op("act", lambda: act_e.activation(out=rows[0:4, 6, :], in_=rows[0:4, 1, :], func=AF.Exp, bias=rows[0:4, 3, LB - 1:LB]),
                 reads=[r_rows], writes=[r_rows])
            if T["kind"] == "p":
                if b + 1 < NB:
                    S.op("dve", lambda: dve_e.tensor_copy(out=m_in[0:4, b + 1:b + 2], in_=rows[0:4, 7, LB - 1:LB]), reads=[r_rows], writes=[r_min])
                else:
                    S.op("dve", lambda: dve_e.tensor_copy(out=m_fin[0:4, 0:1], in_=rows[0:4, 7, LB - 1:LB]), reads=[r_rows], writes=[r_mfin])
                    S.op("dve", lambda: dve_e.tensor_copy(out=m_in[0:4, 0:1], in_=rows[0:4, 7, LB - 1:LB]), reads=[r_rows], writes=[r_min])
            else:
                S.op("dve", lambda: dve_e.tensor_copy(out=m_fin[0:4, b:b + 1], in_=rows[0:4, 7, LB - 1:LB]), reads=[r_rows], writes=[r_mfin])
            S.op("dve", lambda: dve_e.tensor_scalar(out=rexp[0:4, 0:4], in0=bmask[0:4, 0:4], scalar1=rows[0:4, 4, LB - 1:LB], scalar2=None,
                                                    op0=ALU.mult), reads=[r_rows, r_cst], writes=[r_rexp])
            mm_group(PSF(6, 0, 4), [(ones_f[0:4, 0:128], rexp[0:4, 0:4])], [RR("ones_f"), r_rexp], PR(6, 0, 1))
            for i, ri in enumerate((1, 6, 4, 5)):
                transp(PSF(6, 16 + i * 4, 4, LB), rows[0:4, ri, :], ident_f[0:4, 0:4], [r_rows, r_cst], PR(6, 0, 1))
            S.op("dve", lambda: dve_e.tensor_copy(out=decb[:, 0:4], in_=PSF(6, 0, 4)), reads=PR(6, 0, 1), writes=[r_decb])
            S.op("act", lambda: act_e.copy(out=gtm[0:LB, 0:16], in_=PSF(6, 16, 16, LB)), reads=PR(6, 0, 1), writes=[r_gtm])
            for h in range(4):
                for kc in range(2):
                    transp(PSB(7, h * 256 + kc * 128, 128, LB), kT[:, 2 * h + kc, cs], ident_b[:, :], [r_kT[2 * h + kc], RR("ident_b")], PR(7, h, h + 1))
            S.op("dve", lambda: dve_e.tensor_tensor(out=kw[0:LB, :, :], in0=PSB(7, 0, 1024, LB).rearrange("p (a c) -> p a c", a=4),
                                                    in1=gtm[0:LB, 4:8].unsqueeze(2).broadcast_to([LB, 4, 256]), op=ALU.mult),
                 reads=PR(7) + [r_gtm], writes=r_kw)
            for h in range(4):
                bank = 2 + h
                for kc in range(2):
                    mm_group(PSF(bank, kc * 256, 256), [(kw[0:LB, h, kc * 128:(kc + 1) * 128], v_tm[0:LB, b, h * 256:(h + 1) * 256])],
                             [r_kw[h], r_v[b]], PR(bank, kc * 2, kc * 2 + 2))
                    mm_group(PSF(6, 256 + 2 * h + kc, 1), [(kw[0:LB, h, kc * 128:(kc + 1) * 128], ones_b[0:LB, 0:1])], [r_kw[h], RR("ones_b")], PR(6, 2, 3))
            for h in range(4):
                bank = 2 + h
                S.op("dve", lambda h=h, bank=bank: dve_e.scalar_tensor_tensor(
                    out=Cm[:, h, :, :], in0=Cm[:, h, :, :], scalar=decb[:, h:h + 1], in1=PSF(bank, 0, 512).rearrange("p (a b) -> p a b", a=2),
                    op0=ALU.mult, op1=ALU.add), reads=[r_st[h], r_decb] + PR(bank), writes=[r_st[h]])
                S.op("dve", lambda h=h: dve_e.scalar_tensor_tensor(out=nm[:, h, :], in0=nm[:, h, :], scalar=decb[:, h:h + 1], in1=PSF(6, 256 + 2 * h, 2),
                                                                   op0=ALU.mult, op1=ALU.add), reads=[r_st[h], r_decb] + PR(6, 2, 3), writes=[r_st[h]])
            if T["kind"] == "p":
                if b + 1 < NB:
                    q = (b + 1) % 3
                    for h in range(4):
                        S.op("act", lambda h=h: act_e.copy(out=CB[q][:, h, :, :], in_=Cm[:, h, :, :]), reads=[r_st[h]], writes=[r_cbf[q][h]])
                    S.op("act", lambda: act_e.copy(out=NBF[q][:], in_=nm[:]), reads=r_st, writes=[r_nbf[q]])
            else:
                store_state_ml(T, b, trs, o_C_s[b], o_n_s[b])

        def rd(b):
            cs = slice(b * LB, (b + 1) * LB)
            p = b % 2
            rows, gtm = rows2[p], gtm2[p]
            r_rows, r_gtm = r_rows2[p], r_gtm2[p]
            p3 = b % 3
            Cb, nb_ = CB[p3], NBF[p3]
            for h in range(4):
                mm_group(PSF(0, h * 128, LB, LB), [(kT[:, 2 * h + kc, cs], qT[:, 2 * h + kc, cs]) for kc in range(2)],
                         [r_kT[2 * h], r_kT[2 * h + 1], r_qT[2 * h], r_qT[2 * h + 1]], PR(0, h, h + 1))
            for h in range(4):
                mm_group(PSF(1, h * 128, LB, LB), [(selm[0:4, h * 128:h * 128 + LB], rows[0:4, 3, :]), (ident_b[0:LB, 0:LB], maskneg_b[0:LB, 0:LB])],
                         [r_cst, r_rows, RR("ident_b"), RR("maskneg_b")], PR(1, h, h + 1))
            for h in range(4):
                S.op("act", lambda h=h: act_e.activation(out=Dt[0:LB, h, :], in_=PSF(1, h * 128, LB, LB), func=AF.Exp, bias=gtm[0:LB, h:h + 1]),
                     reads=PR(1, h, h + 1) + [r_gtm], writes=[r_Dt[h]])
            S.op("dve", lambda: dve_e.tensor_tensor(out=Pt[0:LB, :, :], in0=PSF(0, 0, 512, LB).rearrange("p (a c) -> p a c", a=4)[:, :, 0:LB],
                                                    in1=Dt[0:LB, :, :], op=ALU.mult), reads=PR(0) + r_Dt, writes=r_Pt)
            for h in range(4):
                bank, off = 4 + h // 2, (h % 2) * 256
                mm_group(PSF(bank, off, 256, LB), [(Pt[0:LB, h, :], v_tm[0:LB, b, h * 256:(h + 1) * 256])], [r_Pt[h], r_v[b]],
                         PR(bank, (h % 2) * 2, (h % 2) * 2 + 2))
                mm_group(PSF(6, 128 + 2 * h + 1, 1, LB), [(Pt[0:LB, h, :], ones_b[0:LB, 0:1])], [r_Pt[h], RR("ones_b")], PR(6, 1, 2))
            for h in range(4):
                bank, off = 2 + h // 2, (h % 2) * 256
                mm_group(PSF(bank, off, 256, LB), [(qT[:, 2 * h + kc, cs], Cb[:, h, kc, :]) for kc in range(2)],
                         [r_qT[2 * h], r_qT[2 * h + 1], r_cbf[p3][h]], PR(bank, (h % 2) * 2, (h % 2) * 2 + 2))
                mm_group(PSF(6, 128 + 2 * h, 1, LB), [(qT[:, 2 * h + kc, cs], nb_[:, h, kc:kc + 1]) for kc in range(2)],
                         [r_qT[2 * h], r_qT[2 * h + 1], r_nbf[p3]], PR(6, 1, 2))
            for i2 in range(2):
                S.op("act", lambda i2=i2: act_e.copy(out=intra_sb[0:LB, 2 * i2:2 * i2 + 2, :], in_=PSF(4 + i2, 0, 512, LB).rearrange("p (a c) -> p a c", a=2)),
                     reads=PR(4 + i2), writes=r_is[2 * i2:2 * i2 + 2])
            S.op("dve", lambda: dve_e.tensor_copy(out=den_sb[0:LB, 0:8], in_=PSF(6, 128, 8, LB)), reads=PR(6, 1, 2), writes=[r_den])
            for h in range(4):
                bank, off = 2 + h // 2, (h % 2) * 256
                S.op("dve", lambda h=h, bank=bank, off=off: dve_e.scalar_tensor_tensor(
                    out=hnum[0:LB, h, :], in0=PSF(bank, off, 256, LB), scalar=gtm[0:LB, 8 + h:9 + h], in1=intra_sb[0:LB, h, :],
                    op0=ALU.mult, op1=ALU.add), reads=PR(bank, (h % 2) * 2, (h % 2) * 2 + 2) + [r_gtm, r_is[h]], writes=[r_hn[h]])
            dnq = den_sb[0:LB, 0:8].rearrange("p (h t) -> p h t", t=2)
            S.op("dve", lambda: dve_e.tensor_tensor(out=sm[0:LB, 0, :], in0=dnq[:, :, 0], in1=gtm[0:LB, 8:12], op=ALU.mult), reads=[r_den, r_gtm], writes=[r_sm])
            S.op("dve", lambda: dve_e.tensor_tensor(out=sm[0:LB, 0, :], in0=sm[0:LB, 0, :], in1=dnq[:, :, 1], op=ALU.add), reads=[r_den, r_sm], writes=[r_sm])
            S.op("act", lambda: act_e.activation(out=sm[0:LB, 0, :], in_=sm[0:LB, 0, :], func=AF.Abs), reads=[r_sm], writes=[r_sm])
            S.op("dve", lambda: dve_e.tensor_tensor(out=sm[0:LB, 0, :], in0=sm[0:LB, 0, :], in1=gtm[0:LB, 12:16], op=ALU.max), reads=[r_sm, r_gtm], writes=[r_sm])
            S.op("dve", lambda: dve_e.reciprocal(out=sm[0:LB, 1, :], in_=sm[0:LB, 0, :]), reads=[r_sm], writes=[r_sm])
            for h in range(4):
                S.op("act", lambda h=h: act_e.activation(out=sqj[0:LB, :], in_=hnum[0:LB, h, :], func=AF.Square, accum_out=sm[0:LB, 2, h:h + 1]),
                     reads=[r_hn[h]], writes=[r_sqj, r_sm])
            S.op("dve", lambda: dve_e.tensor_tensor(out=sm[0:LB, 3, :], in0=sm[0:LB, 1, :], in1=sm[0:LB, 1, :], op=ALU.mult), reads=[r_sm], writes=[r_sm])
            S.op("dve", lambda: dve_e.tensor_tensor(out=sm[0:LB, 3, :], in0=sm[0:LB, 3, :], in1=sm[0:LB, 2, :], op=ALU.mult), reads=[r_sm], writes=[r_sm])
            S.op("act", lambda: act_e.activation(out=sm[0:LB, 3, :], in_=sm[0:LB, 3, :], func=AF.Ln, scale=1.0 / 256.0, bias=eps_t[0:LB, :]),
                 reads=[r_sm, RR("eps")], writes=[r_sm])
            S.op("act", lambda: act_e.activation(out=sm[0:LB, 4, :], in_=sm[0:LB, 3, :], func=AF.Exp, scale=-0.5), reads=[r_sm], writes=[r_sm])
            S.op("dve", lambda: dve_e.tensor_tensor(out=sm[0:LB, 5, :], in0=sm[0:LB, 4, :], in1=sm[0:LB, 1, :], op=ALU.mult), reads=[r_sm], writes=[r_sm])
            for h in range(4):
                S.op("dve", lambda h=h: dve_e.scalar_tensor_tensor(out=hm_tm[0:LB, h * 256:(h + 1) * 256], in0=hnum[0:LB, h, :], scalar=sm[0:LB, 5, h:h + 1],
                                                                   in1=so_tm[0:LB, b, h * 256:(h + 1) * 256], op0=ALU.mult, op1=ALU.mult),
                     reads=[r_hn[h], r_sm, r_so[b]], writes=[r_hm[h]])
            for h in range(4):
                for kc in range(2):
                    transp(PSB(1, h * 256 + kc * 128, LB), hm_tm[0:LB, h * 256 + kc * 128:h * 256 + (kc + 1) * 128], ident_b[0:LB, 0:LB],
                           [r_hm[h], RR("ident_b")], PR(1, h, h + 1))
            copy_op(be(PR(1)), hmgT[:, 0:8, cs], PSB(1, 0, 1024).rearrange("p (a b) -> p a b", a=8)[:, :, 0:LB], PR(1), r_hmg)

        upd(0)
        for b in range(NB):
            if b + 1 < NB:
                upd(b + 1)
            rd(b)
        if T["kind"] == "p":
            for h in range(4):
                S.op("act", lambda h=h: act_e.copy(out=Cbf[:, h, :, :], in_=Cm[:, h, :, :]), reads=[r_st[h]], writes=[r_cbf[0][h]])
            S.op("act", lambda: act_e.copy(out=nbf[:], in_=nm[:]), reads=r_st, writes=[r_nbf[0]])
        if T["kind"] == "p" and T["last"]:
            S.barrier()
            AR.reset(off_tmp)
            trs = AR.alloc((4, 2, 256), F32)
            store_state_ml(T, 0, trs, o_C_p, o_n_p)
        S.barrier()

    def load_state_ml(T, b, trs, Cb, r_cb, nb_, r_nb):
        tid = T["id"]
        r_trs = RR("trs", tid)
        S.dma("sp", "trs", trs[:].rearrange("p h vc k -> p (h vc) k"), cC[b].rearrange("h (vc p) k -> p (h vc) k", p=128), writes=[r_trs])
        for h in range(4):
            for vc in range(2):
                for kc in range(2):
                    transp(PSF(7, (vc * 2 + kc) * 128, 128), trs[:, h, vc, kc * 128:(kc + 1) * 128], ident_f, [r_trs, r_cst], PR(7))
            for vc in range(2):
                copy_op(be(PR(7)), Cm[:, h, :, vc * 128:(vc + 1) * 128], PSF(7, vc * 256, 256).rearrange("p (kc v) -> p kc v", kc=2), PR(7), [r_st[h]])
            S.op("act", lambda h=h: act_e.copy(out=Cb[:, h, :, :], in_=Cm[:, h, :, :]), reads=[r_st[h]], writes=[r_cb[h]])
        for h in range(4):
            S.dma("sp", f"nm_in{h}", nm[:, h, :], cn[b, h].rearrange("(kc p) -> p kc", p=128), writes=[r_st[h]])
        S.op("act", lambda: act_e.copy(out=nb_[:], in_=nm[:]), reads=r_st, writes=[r_nb])
        S.dma("sp", "m_in", m_in[0:4, b:b + 1], cm[b:b + 1, :].rearrange("a h -> h a"), writes=[r_min])

    def store_state_ml(T, b, trs, oC, on):
        tid = T["id"]
        r_trs = RR("trs", tid)
        for h in range(4):
            for vc in range(2):
                for kc in range(2):
                    transp(PSF(7, (vc * 2 + kc) * 128, 128), Cm[:, h, kc, vc * 128:(vc + 1) * 128], ident_f, [r_st[h], r_cst], PR(7))
            copy_op(be(PR(7)), trs[:, h, :, :], PSF(7, 0, 512).rearrange("p (vc k) -> p vc k", vc=2), PR(7), [r_trs])
        S.dma("sp", "trs", oC.rearrange("h (vc p) k -> p (h vc) k", p=128), trs[:].rearrange("p h vc k -> p (h vc) k"), reads=[r_trs])
        for h in range(4):
            S.dma("sp", f"nm_out{h}", on[h].rearrange("(kc p) -> p kc", p=128), nm[:, h, :], reads=[r_st[h]])

    def phase_hgrn(T):
        LB, NB, TT, tid = T["LB"], T["NB"], T["TT"], T["id"]
        AR.reset(0)
        i_tm = AR.alloc((NB, 1024), BF16)
        qhT = AR.alloc((8, TT), BF16)
        off_prods = AR.off
        qt_ = AR.alloc((8, TT), BF16)
        kt_ = AR.alloc((8, TT), BF16)
        qh_ = AR.alloc((8, TT), BF16)
        kh_ = AR.alloc((8, TT), BF16)
        aref = AR.alloc((8, NB, 4), F32)
        off_prep = AR.off
        fbs = [AR.alloc((3, TT), F32) for _ in range(4)]
        ebuf = [AR.alloc((NB, LB), F32) for _ in range(4)]
        r_i = [RR("i_tm", tid, b) for b in range(NB)]
        r_qh = [RR("qhT", tid, j) for j in range(8)]
        r_gs = [RR("gsT", tid, j) for j in range(8)]
        r_o = [RR("o_all", tid, j) for j in range(8)]
        r_hT = R_hT(T)
        hT = GBUF[T["kind"]]["hT"]
        ohgT = GBUF[T["kind"]]["ohgT"]
        bs = [0]

        def nbs():
            bs[0] ^= 1
            return bs[0]
        MID = LB // 2 - 1
        rp = [[RR("hgprod", tid, j, a) for a in range(4)] for j in range(8)]
        r_prod = [None] * 8
        r_aref = [RR("aref", tid, j) for j in range(8)]
        for u in range(2):
            wt, r_w = ws_next()

            def ev(j, ps, prs, u=u):
                copy_op(be(prs), qhT[:, u * 4 + j, :], ps, prs, [r_qh[u * 4 + j]])
            proj_FM(T, wt, r_w, hT, r_hT, list(range(16)), nbs(), ev)

        def fh_unit(u):
            wt, r_w = ws_next()

            def ev(jj, ps, prs, u=u):
                S.op("act", lambda: act_e.activation(out=fbs[jj][:, 0, :], in_=ps, func=AF.Sigmoid), reads=prs, writes=[RR("fbuf", tid, jj, 0)])
            proj_FM(T, wt, r_w, hT, r_hT, list(range(16)), nbs(), ev)

        def prep(u):
            H = range(4)
            J = [4 * u + jj for jj in H]
            rf = [[RR("fbuf", tid, jj, i) for i in range(3)] for jj in H]
            re_ = [RR("ebuf", tid, jj) for jj in H]
            a3 = [fbs[jj][:, 2, :].rearrange("p (a b) -> p a b", a=NB) for jj in H]
            w3 = [fbs[jj][:, 0, :].rearrange("p (a b) -> p a b", a=NB) for jj in H]
            k3 = [fbs[jj][:, 1, :].rearrange("p (a b) -> p a b", a=NB) for jj in H]
            q3 = [qhT[:, J[jj], :].rearrange("p (a b) -> p a b", a=NB) for jj in H]
            for jj in H:
                j = J[jj]
                S.op("dve", lambda jj=jj, j=j: dve_e.tensor_scalar(out=fbs[jj][:, 1, :], in0=fbs[jj][:, 0, :], scalar1=noml_t[:, j:j + 1], scalar2=oml_t[:, j:j + 1],
                                                                   op0=ALU.mult, op1=ALU.add), reads=[rf[jj][0], RR("noml"), RR("oml")], writes=[rf[jj][1]])
            for jj in H:
                j = J[jj]
                S.op("act", lambda jj=jj, j=j: act_e.activation(out=fbs[jj][:, 0, :], in_=fbs[jj][:, 0, :], func=AF.Ln, scale=oml_t[:, j:j + 1], bias=lb_t[:, j:j + 1]),
                     reads=[rf[jj][0], RR("oml"), RR("lb")], writes=[rf[jj][0]])
            for jj in H:
                for b in range(NB):
                    cs = slice(b * LB, (b + 1) * LB)
                    S.op("dve", lambda jj=jj, cs=cs: dve_e.tensor_tensor_scan(out=fbs[jj][:, 2, cs], data0=fbs[jj][:, 0, cs], data1=zeros_f[:, 0:LB], initial=0.0,
                                                                              op0=ALU.add, op1=ALU.add), reads=[rf[jj][0], RR("zeros_f")], writes=[rf[jj][2]])
            for jj in H:
                j = J[jj]
                S.op("dve", lambda jj=jj, j=j: dve_e.tensor_copy(out=aref[:, j, :, 0], in_=a3[jj][:, :, MID]), reads=[rf[jj][2]], writes=[r_aref[j]])
                S.op("dve", lambda jj=jj, j=j: dve_e.tensor_copy(out=aref[:, j, :, 2], in_=a3[jj][:, :, LB - 1]), reads=[rf[jj][2]], writes=[r_aref[j]])
                S.op("act", lambda jj=jj, j=j: act_e.activation(out=aref[:, j, :, 3], in_=a3[jj][:, :, LB - 1], func=AF.Exp), reads=[rf[jj][2]], writes=[r_aref[j]])
            for jj in H:
                j = J[jj]
                S.op("dve", lambda jj=jj, j=j: dve_e.tensor_tensor(out=w3[jj], in0=a3[jj], in1=aref[:, j, :, 0:1].broadcast_to([128, NB, LB]), op=ALU.subtract),
                     reads=[rf[jj][2], r_aref[j], rf[jj][0]], writes=[rf[jj][0]])
            for ai, (dst, src, sres, inn, ires, scale) in enumerate(((qt_, q3, "q", w3, 0, 1.0), (kt_, k3, "k", w3, 0, -1.0), (qh_, q3, "q", a3, 2, 1.0))):
                for jj in H:
                    S.op("act", lambda jj=jj: act_e.activation(out=ebuf[jj][:], in_=inn[jj], func=AF.Exp, scale=scale), reads=[rf[jj][ires]], writes=[re_[jj]])
                for jj in H:
                    j = J[jj]
                    S.op("dve", lambda jj=jj, j=j: dve_e.tensor_tensor(out=dst[:, j, :].rearrange("p (a b) -> p a b", a=NB), in0=src[jj], in1=ebuf[jj][:], op=ALU.mult),
                         reads=[r_qh[j] if sres == "q" else rf[jj][1], re_[jj]], writes=[rp[j][ai]])
            for jj in H:
                j = J[jj]
                S.op("dve", lambda jj=jj, j=j: dve_e.tensor_tensor(out=w3[jj], in0=a3[jj], in1=aref[:, j, :, 2:3].broadcast_to([128, NB, LB]), op=ALU.subtract),
                     reads=[rf[jj][2], r_aref[j], rf[jj][0]], writes=[rf[jj][0]])
            for jj in H:
                S.op("act", lambda jj=jj: act_e.activation(out=ebuf[jj][:], in_=w3[jj], func=AF.Exp, scale=-1.0), reads=[rf[jj][0]], writes=[re_[jj]])
            for jj in H:
                j = J[jj]
                S.op("dve", lambda jj=jj, j=j: dve_e.tensor_tensor(out=kh_[:, j, :].rearrange("p (a b) -> p a b", a=NB), in0=k3[jj], in1=ebuf[jj][:], op=ALU.mult),
                     reads=[rf[jj][1], re_[jj]], writes=[rp[j][3]])

        fh_unit(0)
        prep(0)
        fh_unit(1)
        for u in range(2):
            wt, r_w = ws_next()

            def ev(b, ps, prs, u=u):
                copy_op(be(prs), i_tm[0:LB, b, u * 512:(u + 1) * 512], ps, prs, [r_i[b]])
            proj_TM(T, wt, r_w, hT, r_hT, 16, nbs(), ev)
        Pt = AR.alloc((4 * NB, LB), BF16)
        khtm = AR.alloc((4 * NB, 128), BF16)
        r_Pt = [RR("hPt", tid, b) for b in range(NB)]
        r_kh = [RR("khtm", tid, b) for b in range(NB)]

        def rec_a(u):
            for b in range(NB):
                S.op("dve", lambda b=b: dve_e.memset(PSF(b, 0, 512), 0.0), writes=PR(b))
            for b in range(NB):
                cs = slice(b * LB, (b + 1) * LB)
                for jj in range(4):
                    j = 4 * u + jj
                    if LB == 128:
                        Hh = 64
                        c0 = b * LB
                        mm_group(PSF(b, jj * 128, Hh, Hh), [(kt_[:, j, c0:c0 + Hh], qt_[:, j, c0:c0 + Hh])], rp[j], PR(b))
                        mm_group(PSF(b, jj * 128 + Hh, Hh, LB), [(kt_[:, j, cs], qt_[:, j, c0 + Hh:c0 + LB])], rp[j], PR(b))
                    else:
                        mm_group(PSF(b, jj * 128, LB, LB), [(kt_[:, j, cs], qt_[:, j, cs])], rp[j], PR(b))
            for b in range(NB):
                S.op("dve", lambda b=b: dve_e.tensor_tensor(out=Pt[0:LB, b * 4:b * 4 + 4, :], in0=PSF(b, 0, 512, LB).rearrange("p (a c) -> p a c", a=4)[:, :, 0:LB],
                                                            in1=mask4[0:LB, :, 0:LB], op=ALU.mult),
                     reads=PR(b) + [RR("mask4")], writes=[r_Pt[b]])
            for b in range(NB):
                cs = slice(b * LB, (b + 1) * LB)
                bank = 4 + b // 2
                for jj in range(4):
                    j = 4 * u + jj
                    transp(PSB(bank, ((b % 2) * 4 + jj) * 128, 128, LB), kh_[:, j, cs], ident_b[:, :], rp[j] + [RR("ident_b")], PR(bank))
            for bank in range(4, 4 + (NB + 1) // 2):
                nb_here = min(2, NB - (bank - 4) * 2)
                b0 = (bank - 4) * 2
                copy_op("act", khtm[0:LB, b0 * 4:(b0 + nb_here) * 4, :], PSB(bank, 0, nb_here * 512, LB).rearrange("p (a c) -> p a c", c=128),
                        PR(bank), [r_kh[b0 + i] for i in range(nb_here)])
            for b in range(NB):
                for jj in range(4):
                    j = 4 * u + jj
                    mm_group(PSF(b, jj * 128, 128), [(khtm[0:LB, b * 4 + jj, :], i_tm[0:LB, b, j * 128:(j + 1) * 128])], [r_kh[b], r_i[b]], PR(b))

        def rec_b(u):
            for b in range(NB):
                cs = slice(b * LB, (b + 1) * LB)
                ob = 6 + b % 2
                if T["kind"] == "s":
                    S.dma("sp", "S_in", Sm[:, 4 * u:4 * u + 4, :], cS[b, 4 * u:4 * u + 4].rearrange("j c v -> c j v"), writes=r_S[4 * u:4 * u + 4])
                    S.op("act", lambda: act_e.copy(out=Sbf[:, 4 * u:4 * u + 4, :], in_=Sm[:, 4 * u:4 * u + 4, :]), reads=r_S[4 * u:4 * u + 4], writes=r_S[4 * u:4 * u + 4])
                for jj in range(4):
                    j = 4 * u + jj
                    mm_group(PSF(ob, jj * 128, LB), [(i_tm[0:LB, b, j * 128:(j + 1) * 128], Pt[0:LB, b * 4 + jj, :]), (Sbf[:, j, :], qh_[:, j, cs])],
                             [r_i[b], r_Pt[b], r_S[j]] + rp[j], PR(ob))
                copy_op(be(PR(ob)), o_all[:, 4 * u:4 * u + 4, cs], PSF(ob, 0, 512).rearrange("p (a c) -> p a c", a=4)[:, :, 0:LB], PR(ob), r_o[4 * u:4 * u + 4])
                for jj in range(4):
                    j = 4 * u + jj
                    S.op("dve", lambda j=j, jj=jj: dve_e.scalar_tensor_tensor(out=Sm[:, j, :], in0=Sm[:, j, :], scalar=aref[:, j, b, 3:4], in1=PSF(b, jj * 128, 128),
                                                                              op0=ALU.mult, op1=ALU.add),
                         reads=[r_S[j], r_aref[j]] + PR(b), writes=[r_S[j]])
                S.op("act", lambda: act_e.copy(out=Sbf[:, 4 * u:4 * u + 4, :], in_=Sm[:, 4 * u:4 * u + 4, :]), reads=r_S[4 * u:4 * u + 4], writes=r_S[4 * u:4 * u + 4])
                if T["kind"] == "s":
                    S.dma("sp", "S_out", o_S_s[b, 4 * u:4 * u + 4].rearrange("j c v -> c j v"), Sm[:, 4 * u:4 * u + 4, :], reads=r_S[4 * u:4 * u + 4])

        rec_a(0)
        prep(1)
        S.barrier()
        AR.reset(off_prep)
        o_all = AR.alloc((8, TT), F32)
        rec_b(0)
        rec_a(1)
        rec_b(1)
        if T["kind"] == "p" and T["last"]:
            S.dma("sp", "S_out", o_S_p.rearrange("j c v -> c j v"), Sm[:], reads=r_S)
        S.barrier()
        AR.reset(off_prods)
        gsT = AR.alloc((8, TT), BF16)
        sgt = [AR.alloc((TT,), F32) for _ in range(2)]
        k = [0]
        for u in range(2):
            wt, r_w = ws_next()

            def ev(j, ps, prs, u=u):
                k[0] ^= 1
                t, rt = sgt[k[0]], RR("sgt", tid, k[0])
                S.op("act", lambda: act_e.activation(out=t[:], in_=ps, func=AF.Sigmoid), reads=prs, writes=[rt])
                S.op("dve", lambda: dve_e.tensor_tensor(out=gsT[:, u * 4 + j, :], in0=ps, in1=t[:], op=ALU.mult), reads=prs + [rt], writes=[r_gs[u * 4 + j]])
            proj_FM(T, wt, r_w, hT, r_hT, list(range(16)), nbs(), ev)
        osq = [AR.alloc((TT,), F32) for _ in range(2)]
        rstd_b = AR.alloc((TT,), F32)
        assert AR.off <= off_prep, ("hgrn tail scratch overlaps o_all", AR.off, off_prep)
        r_rstd = RR("rstd_b", tid)
        for j in range(8):
            ro = RR("osq", tid, j % 2)
            S.op("act", lambda j=j: act_e.activation(out=osq[j % 2][:], in_=o_all[:, j, :], func=AF.Square), reads=[r_o[j]], writes=[ro])
            S.op("pe", lambda j=j: pe_e.matmul(PSF(7, 0, TT), lhsT=ones_f[:, :], rhs=osq[j % 2][:], start=(j == 0), stop=(j == 7)),
                 reads=[ro, RR("ones_f")], writes=PR(7))
        S.op("act", lambda: act_e.activation(out=rstd_b[:], in_=PSF(7, 0, TT), func=AF.Ln, scale=1.0 / 1024.0, bias=eps_t[:, :]),
             reads=PR(7) + [RR("eps")], writes=[r_rstd])
        S.op("act", lambda: act_e.activation(out=rstd_b[:], in_=rstd_b[:], func=AF.Exp, scale=-0.5), reads=[r_rstd], writes=[r_rstd])
        r_ohg = R_ohg(T)
        for j in range(8):
            ro = RR("osq", tid, j % 2)
            S.op("dve", lambda j=j: dve_e.tensor_tensor(out=osq[j % 2][:], in0=o_all[:, j, :], in1=rstd_b[:], op=ALU.mult), reads=[r_o[j], r_rstd], writes=[ro])
            S.op("dve", lambda j=j: dve_e.scalar_tensor_tensor(out=ohgT[:, j, 0:TT], in0=osq[j % 2][:], scalar=par[:, PO_GHN + j:PO_GHN + j + 1], in1=gsT[:, j, :],
                                                               op0=ALU.mult, op1=ALU.mult), reads=[ro, r_par, r_gs[j]], writes=[r_ohg[j]])
        S.barrier()

    def phase_uw(ctxs):
        for c in ctxs:
            T = c["T"]
            TT, tid = T["TT"], T["id"]
            c["uT"] = AR.alloc((16, TT), BF16)
            c["sab"] = [AR.alloc((4, TT), F32) for _ in range(2)]
            c["tmp"] = [AR.alloc((TT,), F32) for _ in range(2)]
            c["r_u"] = [RR("uT", tid, i) for i in range(16)]
        bs = [0]
        for q in range(4):
            for gi in range(2):
                wt, r_w = ws_next()
                for c in ctxs:
                    T = c["T"]
                    bs[0] ^= 1

                    def ev(j, ps, prs, gi=gi, c=c, T=T):
                        S.op("act", lambda: act_e.activation(out=c["sab"][gi][:, j, :], in_=ps, func=AF.Sigmoid), reads=prs, writes=[RR("sab", T["id"], gi, j)])
                    proj_FM(T, wt, r_w, GBUF[T["kind"]]["hT"], R_hT(T), list(range(16)), bs[0], ev)
            wbab, r_wbab = ws_next()
            for c in ctxs:
                T = c["T"]
                TT, tid = T["TT"], T["id"]
                hmgT, ohgT = GBUF[T["kind"]]["hmgT"], GBUF[T["kind"]]["ohgT"]
                uT, sab, tmp, r_u = c["uT"], c["sab"], c["tmp"], c["r_u"]
                for j in range(4):
                    jc = q * 4 + j
                    bs[0] ^= 1
                    ba, bb_ = bs[0] * 4 + (j % 2) * 2, bs[0] * 4 + (j % 2) * 2 + 1
                    mm_group(PSF(ba, 0, TT), [(wbab[:, kc, j * 128:(j + 1) * 128], hmgT[:, kc, 0:TT]) for kc in range(8)], [r_wbab] + R_hmg(T), PR(ba))
                    mm_group(PSF(bb_, 0, TT), [(wbab[:, 8 + kc, j * 128:(j + 1) * 128], ohgT[:, kc, 0:TT]) for kc in range(8)], [r_wbab] + R_ohg(T), PR(bb_))
                    rt = [RR("uw_tmp", tid, i) for i in range(2)]
                    S.op("dve", lambda: dve_e.tensor_tensor(out=tmp[0][:], in0=PSF(ba, 0, TT), in1=sab[0][:, j, :], op=ALU.mult),
                         reads=PR(ba) + [RR("sab", tid, 0, j)], writes=[rt[0]])
                    S.op("dve", lambda: dve_e.tensor_tensor(out=tmp[1][:], in0=PSF(bb_, 0, TT), in1=sab[1][:, j, :], op=ALU.mult),
                         reads=PR(bb_) + [RR("sab", tid, 1, j)], writes=[rt[1]])
                    S.op("dve", lambda: dve_e.tensor_tensor(out=uT[:, jc, :], in0=tmp[0][:], in1=tmp[1][:], op=ALU.add), reads=rt, writes=[r_u[jc]])
        for nb in range(4):
            wt, r_w = ws_next()
            for c in ctxs:
                T = c["T"]
                LB = T["LB"]
                x_tm, r_x = c["x_tm"], c["r_x"]

                def ev(b, ps, prs, nb=nb, x_tm=x_tm, r_x=r_x, LB=LB):
                    S.op("dve", lambda: dve_e.tensor_tensor(out=x_tm[0:LB, b, nb * 512:(nb + 1) * 512], in0=ps, in1=x_tm[0:LB, b, nb * 512:(nb + 1) * 512], op=ALU.add),
                         reads=prs + [r_x[b]], writes=[r_x[b]])
                bs[0] ^= 1
                proj_TM(T, wt, r_w, c["uT"], c["r_u"], 16, bs[0], ev)
        S.barrier()

    def phase_ff(ctxs, next_T=None):
        off0 = AR.off
        for c in ctxs:
            T = c["T"]
            c["aT"] = AR.alloc((16, T["TT"]), BF16)
            c["sq"] = [AR.alloc((T["TT"],), F32) for _ in range(2)]
            c["r_a"] = [RR("aT", T["id"], i) for i in range(16)]
        k = [0]
        bs = [0]
        for part in range(NPART):
            for i in range(4):
                wt, r_w = ws_next()
                for c in ctxs:
                    T = c["T"]

                    def ev(j, ps, prs, i=i, c=c, T=T):
                        k[0] ^= 1
                        t, rt = c["sq"][k[0]], RR("ffsq", T["id"], k[0])
                        S.op("act", lambda: act_e.activation(out=t[:], in_=ps, func=AF.Square), reads=prs, writes=[rt])
                        S.op("dve", lambda: dve_e.scalar_tensor_tensor(out=c["aT"][:, i * 4 + j, :], in0=ps, scalar=0.0, in1=t[:], op0=ALU.is_gt, op1=ALU.mult),
                             reads=prs + [rt], writes=[c["r_a"][i * 4 + j]])
                    bs[0] ^= 1
                    proj_FM(T, wt, r_w, GBUF[T["kind"]]["hT"], R_hT(T), list(range(16)), bs[0], ev)
            for nb in range(4):
                wt, r_w = ws_next()
                for c in ctxs:
                    T = c["T"]
                    LB, NB = T["LB"], T["NB"]
                    x_tm, r_x, aT, r_a = c["x_tm"], c["r_x"], c["aT"], c["r_a"]
                    bs[0] ^= 1
                    for b in range(NB):
                        bank = bs[0] * 4 + b
                        mm_group(PSF(bank, 0, 512, LB), [(aT[:, kc, b * LB:(b + 1) * LB], wt[:, kc, :]) for kc in range(16)], [r_w] + r_a, PR(bank))
                        S.op("dve", lambda: dve_e.tensor_tensor(out=x_tm[0:LB, b, nb * 512:(nb + 1) * 512], in0=PSF(bank, 0, 512, LB),
                                                                in1=x_tm[0:LB, b, nb * 512:(nb + 1) * 512], op=ALU.add),
                             reads=PR(bank) + [r_x[b]], writes=[r_x[b]])
        S.barrier()
        AR.reset(off0)
        if next_T is not None:
            nLB, nNB = next_T["LB"], next_T["NB"]
            nx = AR.alloc((nNB, D), F32)
            nr = [RR("x_tm", next_T["id"], "n1", b) for b in range(nNB)]
            nsrc = xp[next_T["id"] * 512:(next_T["id"] + 1) * 512, :]
            for b in range(nNB):
                S.dma("sp", f"xn{b}", nx[0:nLB, b, :], nsrc[b * nLB:(b + 1) * nLB, :], writes=[nr[b]])
            next_T["xpre"] = (nx, nr)
        junk = AR.alloc((2048,), BF16)
        ssq = AR.alloc((8,), F32)
        gfin_b = AR.alloc((2048,), F32)
        r_gfin = RR("gfin_l", ctxs[0]["T"]["id"])
        S.dma("sp", "gfin", gfin_b[:], gfin_d.partition_broadcast(128), writes=[r_gfin])
        if next_T is not None:
            next_T["xpre_off"] = AR.off
        for ci, c in enumerate(ctxs):
            T = c["T"]
            LB, NB, tid = T["LB"], T["NB"], T["id"]
            x_tm, r_x, ydst = c["x_tm"], c["r_x"], c["ydst"]
            for b in range(NB):
                sc = ci * 4 + b
                xb = x_tm[0:LB, b, :]
                rs, rj = RR("f_ssq", sc), RR("f_junk")
                S.op("act", lambda: act_e.activation(out=junk[0:LB, :], in_=xb, func=AF.Square, accum_out=ssq[0:LB, sc:sc + 1]), reads=[r_x[b]], writes=[rj, rs])
                S.op("act", lambda: act_e.activation(out=ssq[0:LB, sc:sc + 1], in_=ssq[0:LB, sc:sc + 1], func=AF.Ln, scale=1.0 / D, bias=eps_t[0:LB, :]),
                     reads=[rs, RR("eps")], writes=[rs])
                S.op("act", lambda: act_e.activation(out=ssq[0:LB, sc:sc + 1], in_=ssq[0:LB, sc:sc + 1], func=AF.Exp, scale=-0.5), reads=[rs], writes=[rs])
                S.op("dve", lambda: dve_e.scalar_tensor_tensor(out=xb, in0=xb, scalar=ssq[0:LB, sc:sc + 1], in1=gfin_b[0:LB, :], op0=ALU.mult, op1=ALU.mult),
                     reads=[r_x[b], rs, r_gfin], writes=[r_x[b]])
                S.dma("sp", f"y{T['kind']}{b}", ydst[b * LB:(b + 1) * LB, :], xb, reads=[r_x[b]])
        if next_T is None:
            S.barrier()

    tiles = [dict(kind="p", id=t, LB=128, NB=4, TT=512, first=(t == 0), last=(t == 3)) for t in range(4)]
    tiles.append(dict(kind="s", id=4, LB=32, NB=2, TT=64, first=True, last=True))
    if tile_sel is not None:
        tiles = [t for t in tiles if t["id"] in tile_sel]

    def front(T):
        LB, NB, TT = T["LB"], T["NB"], T["TT"]
        if T["kind"] == "p":
            xsrc = xp[T["id"] * 512:(T["id"] + 1) * 512, :]
        else:
            xsrc = xs
            for b in range(NB):
                for jx in range(3):
                    S.dma("sp", "hist_in", hist[:, :, b, jx], cconv[b, jx].rearrange("(kc p) -> p kc", p=128), writes=[r_hist])
        if "xpre" in T:
            x_tm, r_x = T["xpre"]
            AR.reset(T["xpre_off"])
        else:
            AR.reset(0)
            x_tm = AR.alloc((NB, D), F32)
            r_x = [RR("x_tm", T["id"], "n1", b) for b in range(NB)]
            for b in range(NB):
                S.dma("sp", f"x{b}", x_tm[0:LB, b, :], xsrc[b * LB:(b + 1) * LB, :], writes=[r_x[b]])
        rmsnorm_to_FM(T, x_tm, r_x, PO_GMIX, GBUF[T["kind"]]["hT"], R_hT(T))
        S.barrier()
        phase_mlstm(T)
        phase_hgrn(T)
        if T["kind"] == "s":
            for b in range(NB):
                for jx in range(3):
                    S.dma("sp", "tail_out", o_conv_s[b, jx].rearrange("(kc p) -> p kc", p=128), tail[:, :, b, jx], reads=[r_tail])
            S.dma("sp", "m_out", o_m_s.rearrange("b h -> h b"), m_fin[0:4, 0:2], reads=[r_mfin])
        elif T["last"]:
            for jx in range(3):
                S.dma("sp", "tail_out", o_conv_p[jx].rearrange("(kc p) -> p kc", p=128), tail[:, :, 3, jx], reads=[r_tail])
            S.dma("sp", "m_out", o_m_p.rearrange("(h a) -> h a", a=1), m_fin[0:4, 0:1], reads=[r_mfin])

    def back(Ts, next_T=None):
        AR.reset(0)
        ctxs = []
        for T0 in Ts:
            if T0["kind"] == "p":
                T = T0
                xsrc = xp[T["id"] * 512:(T["id"] + 1) * 512, :]
                ydst = yp[T["id"] * 512:(T["id"] + 1) * 512, :]
            else:
                T = dict(kind="s", id=5, LB=64, NB=1, TT=64, first=True, last=True)
                xsrc, ydst = xs, ys
            LB, NB = T["LB"], T["NB"]
            x_tm = AR.alloc((NB, D), F32)
            r_x = [RR("x_tm", T["id"], "uw", b) for b in range(NB)]
            for b in range(NB):
                S.dma("sp", f"x{T['kind']}{b}", x_tm[0:LB, b, :], xsrc[b * LB:(b + 1) * LB, :], writes=[r_x[b]])
            ctxs.append(dict(T=T, x_tm=x_tm, r_x=r_x, ydst=ydst))
        off_after_x = AR.off
        phase_uw(ctxs)
        AR.reset(off_after_x)
        for c in ctxs:
            rmsnorm_to_FM(c["T"], c["x_tm"], c["r_x"], PO_GFFN, GBUF[c["T"]["kind"]]["hT"], R_hT(c["T"]), keep=True)
        S.barrier()
        AR.reset(off_after_x)
        phase_ff(ctxs, next_T)

    ids = [t["id"] for t in tiles]
    groups = []
    for T in tiles:
        if T["id"] == 4 and 3 in ids:
            groups[-1].append(T)
        else:
            groups.append([T])
    for gi, g in enumerate(groups):
        for T in g:
            front(T)
        nxt = groups[gi + 1][0] if gi + 1 < len(groups) else None
        back(g, nxt if (nxt is not None and nxt["kind"] == "p" and len(g) == 1) else None)
    S.barrier(final=True)
    assert ws["next"] == len(unit_specs), (ws["next"], len(unit_specs))
    es.close()
    return nc


def _consts():
    c = np.zeros((128, CW), np.float32)
    c[:, CO_ID:CO_ID + 128] = np.eye(128, dtype=np.float32)
    s = np.arange(128)[:, None]
    l = np.arange(128)[None, :]
    c[:, CO_MNEG:CO_MNEG + 128] = np.where(s <= l, 0.0, -30000.0)
    c[:, CO_M01:CO_M01 + 128] = (s <= l).astype(np.float32)
    for h in range(4):
        c[h, CO_SEL + h * 128:CO_SEL + (h + 1) * 128] = 1.0
        c[h, CO_BM + h] = 1.0
    return c


_NC_CACHE = {}


def kernel(x_prompt, x_sample, cache_mlstm_conv, state_mlstm_C, state_mlstm_n, state_mlstm_m, state_hgrn_S,
           g_mix, w_in, b_if, w_conv, b_conv, g_mnorm, g_hnorm, hgrn_lb_logits, w_branch_a, w_branch_b,
           w_out, g_ffn, w_ff1, w_ff2, g_final):
    f = lambda a: np.ascontiguousarray(np.asarray(a, dtype=np.float32))
    par = np.zeros((128, PW), np.float32)
    par[:, PO_GMIX:PO_GMIX + 16] = f(g_mix)[0].reshape(16, 128).T
    par[:, PO_GFFN:PO_GFFN + 16] = f(g_ffn)[0].reshape(16, 128).T
    par[:, PO_GHN:PO_GHN + 8] = f(g_hnorm)[0].reshape(8, 128).T
    par[:, PO_WCONV:PO_WCONV + 64] = f(w_conv)[0].reshape(4, 16, 128).transpose(2, 1, 0).reshape(128, 64)
    par[:, PO_BCONV:PO_BCONV + 16] = f(b_conv)[0].reshape(16, 128).T
    lbl = f(hgrn_lb_logits)
    par[:, PO_LBL:PO_LBL + 8] = lbl[0].reshape(8, 128).T
    par[:, PO_LBL + 8:PO_LBL + 16] = lbl[1].reshape(8, 128).T
    par[0:4, PO_BIF] = f(b_if)[0, 0:4]
    par[0:4, PO_BIF + 1] = f(b_if)[0, 4:8]
    consts = _consts()
    shared = {
        "w_in": f(w_in)[0], "w_ba": f(w_branch_a)[0], "w_bb": f(w_branch_b)[0], "w_out": f(w_out)[0],
        "w_ff1": f(w_ff1)[0], "w_ff2": f(w_ff2)[0], "gmn": f(g_mnorm)[0].reshape(1024), "gfin": f(g_final),
        "consts": consts, "params": par,
    }
    xpn, xsn = f(x_prompt), f(x_sample)
    cc, cCn, cnn, cmn, cSn = f(cache_mlstm_conv)[0], f(state_mlstm_C)[0], f(state_mlstm_n)[0], f(state_mlstm_m)[0], f(state_hgrn_S)[0]
    in_maps = []
    for c in range(NCORE):
        m = dict(shared)
        m["xp"] = xpn[c]
        m["xs"] = np.ascontiguousarray(xsn[2 * c:2 * c + 2].reshape(64, D))
        m["cconv"] = np.ascontiguousarray(cc[2 * c:2 * c + 2])
        m["cC"] = np.ascontiguousarray(cCn[2 * c:2 * c + 2])
        m["cn"] = np.ascontiguousarray(cnn[2 * c:2 * c + 2])
        m["cm"] = np.ascontiguousarray(cmn[2 * c:2 * c + 2])
        m["cS"] = np.ascontiguousarray(cSn[2 * c:2 * c + 2])
        in_maps.append(m)
    if "nc" not in _NC_CACHE:
        _NC_CACHE["nc"] = build_program()
    nc = _NC_CACHE["nc"]
    res = run_bass_kernel_spmd(nc, in_maps, core_ids=list(range(NCORE)))
    rs = res.results
    g = lambda k: np.stack([np.asarray(r[k], dtype=np.float32) for r in rs])
    y_prompt = g("yp").reshape(8, SEQ, D)
    y_sample = g("ys").reshape(8, 2, 32, D).reshape(16, 32, D)
    out = (
        y_prompt, y_sample,
        g("o_conv_p").reshape(1, 8, 3, D), g("o_C_p").reshape(1, 8, 4, 256, 256), g("o_n_p").reshape(1, 8, 4, 256),
        g("o_m_p").reshape(1, 8, 4), g("o_S_p").reshape(1, 8, 8, 128, 128),
        g("o_conv_s").reshape(1, 16, 3, D), g("o_C_s").reshape(1, 16, 4, 256, 256), g("o_n_s").reshape(1, 16, 4, 256),
        g("o_m_s").reshape(1, 16, 4), g("o_S_s").reshape(1, 16, 8, 128, 128),
    )
    return out
```
